# Optimizing a Trainium2 kernel written in Bass

```python
import jax, jax.numpy as jnp
from jax import lax
import numpy as np

D_MODEL = 1024
BATCH = 8
SEQ = 2048
DEPTH = 1
DEC_BATCH = 128
DEC_SEQ = 4
PAST_LEN = 16384
PAGE_SIZE = 128

D_MIX = D_MODEL
D_RWKV = D_MIX // 2
D_CONV = D_MIX - D_RWKV
HEAD_SIZE = 64
N_HEADS_RWKV = D_RWKV // HEAD_SIZE
LORA_W = 64
LORA_A = 64
LORA_G = 128
CONV_W = 3
D_FF = 2816
ALPHA = (2.0 * DEPTH) ** 0.25
BETA = (8.0 * DEPTH) ** -0.25
LN_EPS = 1e-5
GN_EPS = 1e-5 * HEAD_SIZE

OFF_R = 0
OFF_K = OFF_R + D_RWKV
OFF_V = OFF_K + D_RWKV
OFF_WD = OFF_V + D_RWKV
OFF_AD = OFF_WD + LORA_W
OFF_GD = OFF_AD + LORA_A
P_SHIFT = OFF_GD + LORA_G
OFF_CB = P_SHIFT
OFF_CC = OFF_CB + D_CONV
OFF_CH = OFF_CC + D_CONV
P_TOTAL = OFF_CH + D_CONV

kernel_name = "hymba_rwkv7_shortconv_macaron_deepnorm_step"


def layer_norm(x, g, b):
    xf = x.astype(jnp.float32)
    mu = jnp.mean(xf, axis=-1, keepdims=True)
    var = jnp.mean(jnp.square(xf - mu), axis=-1, keepdims=True)
    return ((xf - mu) * lax.rsqrt(var + LN_EPS) * g.astype(jnp.float32) + b.astype(jnp.float32)).astype(x.dtype)


def swiglu(x, wg, wu, wd):
    return (jax.nn.silu(x @ wg) * (x @ wu)) @ wd


def wkv7_recurrence(r, w, k, v, kk, a, s0):
    def step(s, inp):
        r_t, w_t, k_t, v_t, kk_t, a_t = inp
        sa = jnp.einsum("bhij,bhj->bhi", s, -kk_t)
        s = (s * w_t[:, :, None, :]
             + sa[..., None] * (kk_t * a_t)[:, :, None, :]
             + v_t[..., None] * k_t[:, :, None, :])
        o_t = jnp.einsum("bhij,bhj->bhi", s, r_t)
        return s, o_t
    xs = (jnp.moveaxis(r, 1, 0), jnp.moveaxis(w, 1, 0), jnp.moveaxis(k, 1, 0),
          jnp.moveaxis(v, 1, 0), jnp.moveaxis(kk, 1, 0), jnp.moveaxis(a, 1, 0))
    s_final, o = lax.scan(step, s0, xs)
    return jnp.moveaxis(o, 0, 1), s_final


def trunk_layer(x, wkv0, shift0, conv0, p):
    f32 = jnp.float32
    bsz, T, _ = x.shape
    x = layer_norm(ALPHA * x + 0.5 * swiglu(x, p["ffn1_wg"], p["ffn1_wu"], p["ffn1_wd"]), p["ln1_g"], p["ln1_b"])

    proj = x @ p["w_in"]

    ps = proj[..., :P_SHIFT]
    prev = jnp.concatenate([shift0[:, None, :].astype(ps.dtype), ps[:, :-1]], axis=1)
    ps_mix = ps + p["mu_shift"] * (prev - ps)
    new_shift = ps[:, -1]
    r = ps_mix[..., OFF_R:OFF_K]
    k = ps_mix[..., OFF_K:OFF_V]
    v = ps_mix[..., OFF_V:OFF_WD]
    wd = ps_mix[..., OFF_WD:OFF_AD]
    ad = ps_mix[..., OFF_AD:OFF_GD]
    gd = ps_mix[..., OFF_GD:P_SHIFT]

    w_log = -jax.nn.softplus(-(p["w0"] + jnp.tanh(wd) @ p["w_lora_up"]).astype(f32)) - 0.5
    decay = jnp.exp(-jnp.exp(w_log))
    a = jax.nn.sigmoid((p["a0"] + ad @ p["a_lora_up"]).astype(f32))
    g = jax.nn.sigmoid(gd) @ p["g_lora_up"]

    def heads(t):
        return t.astype(f32).reshape(bsz, T, N_HEADS_RWKV, HEAD_SIZE)

    def headvec(t):
        return t.astype(f32).reshape(N_HEADS_RWKV, HEAD_SIZE)

    r_h, k_h, v_h, a_h, w_h = heads(r), heads(k), heads(v), heads(a), heads(decay)
    kk = k_h * headvec(p["k_k"])
    kk = kk / jnp.maximum(jnp.sqrt(jnp.sum(kk * kk, axis=-1, keepdims=True)), 1e-12)
    k_h = k_h * (1.0 + (a_h - 1.0) * headvec(p["k_a"]))

    o, wkv_new = wkv7_recurrence(r_h, w_h, k_h, v_h, kk, a_h, wkv0.astype(f32))
    o_mu = jnp.mean(o, axis=-1, keepdims=True)
    o_var = jnp.mean(jnp.square(o - o_mu), axis=-1, keepdims=True)
    o = (o - o_mu) * lax.rsqrt(o_var + GN_EPS) * headvec(p["lnx_g"]) + headvec(p["lnx_b"])
    o = o + jnp.sum(r_h * k_h * p["r_k"].astype(f32), axis=-1, keepdims=True) * v_h
    y_rwkv = o.reshape(bsz, T, D_RWKV).astype(x.dtype) * g

    c_b = proj[..., OFF_CB:OFF_CC]
    c_c = proj[..., OFF_CC:OFF_CH]
    c_h = proj[..., OFF_CH:P_TOTAL]
    u = c_c * c_h
    u_full = jnp.concatenate([conv0.astype(u.dtype), u], axis=1)
    cw = p["conv_w"]
    z = cw[0] * u_full[:, 0:T]
    for i in range(1, CONV_W):
        z = z + cw[i] * u_full[:, i:i + T]
    new_conv = u_full[:, T:]
    y_conv = c_b * z

    mix = jnp.concatenate([y_rwkv, y_conv], axis=-1) @ p["w_out"]
    x = layer_norm(ALPHA * x + mix, p["ln2_g"], p["ln2_b"])
    x = layer_norm(ALPHA * x + 0.5 * swiglu(x, p["ffn2_wg"], p["ffn2_wu"], p["ffn2_wd"]), p["ln3_g"], p["ln3_b"])
    return x, wkv_new, new_shift, new_conv


def setup_inputs(seed: int = 0) -> dict:
    key = jax.random.key(seed)
    ks = iter(jax.random.split(key, 40))

    def nrm(shape, scale):
        return jax.random.normal(next(ks), shape, jnp.float32) * scale

    d = D_MODEL
    col_scale = jnp.ones((P_TOTAL,), jnp.float32).at[OFF_V:OFF_WD].set(BETA)
    inp = {}
    inp["x_prompt"] = nrm((BATCH, SEQ, d), 1.0)
    inp["x_sample"] = nrm((DEC_BATCH, DEC_SEQ, d), 1.0)
    inp["state_wkv"] = nrm((DEPTH, DEC_BATCH, N_HEADS_RWKV, HEAD_SIZE, HEAD_SIZE), 0.5)
    inp["state_shift"] = nrm((DEPTH, DEC_BATCH, P_SHIFT), 1.0)
    inp["state_conv"] = nrm((DEPTH, DEC_BATCH, CONV_W - 1, D_CONV), 1.0)
    inp["ln1_g"] = 1.0 + nrm((DEPTH, d), 0.05)
    inp["ln1_b"] = nrm((DEPTH, d), 0.02)
    inp["ffn1_wg"] = nrm((DEPTH, d, D_FF), d ** -0.5)
    inp["ffn1_wu"] = nrm((DEPTH, d, D_FF), d ** -0.5)
    inp["ffn1_wd"] = nrm((DEPTH, D_FF, d), BETA * D_FF ** -0.5)
    inp["w_in"] = nrm((DEPTH, d, P_TOTAL), d ** -0.5) * col_scale
    inp["mu_shift"] = jax.random.uniform(next(ks), (DEPTH, P_SHIFT), jnp.float32)
    inp["w0"] = -3.0 + nrm((DEPTH, D_RWKV), 1.0)
    inp["w_lora_up"] = nrm((DEPTH, LORA_W, D_RWKV), 0.5 * LORA_W ** -0.5)
    inp["a0"] = nrm((DEPTH, D_RWKV), 0.3)
    inp["a_lora_up"] = nrm((DEPTH, LORA_A, D_RWKV), 0.5 * LORA_A ** -0.5)
    inp["g_lora_up"] = nrm((DEPTH, LORA_G, D_RWKV), LORA_G ** -0.5)
    inp["k_k"] = 0.85 + nrm((DEPTH, D_RWKV), 0.05)
    inp["k_a"] = 1.0 + nrm((DEPTH, D_RWKV), 0.05)
    inp["r_k"] = nrm((DEPTH, N_HEADS_RWKV, HEAD_SIZE), 0.1)
    inp["lnx_g"] = 1.0 + nrm((DEPTH, D_RWKV), 0.05)
    inp["lnx_b"] = nrm((DEPTH, D_RWKV), 0.02)
    inp["conv_w"] = nrm((DEPTH, CONV_W, D_CONV), CONV_W ** -0.5)
    inp["w_out"] = nrm((DEPTH, D_MIX, d), BETA * D_MIX ** -0.5)
    inp["ln2_g"] = 1.0 + nrm((DEPTH, d), 0.05)
    inp["ln2_b"] = nrm((DEPTH, d), 0.02)
    inp["ffn2_wg"] = nrm((DEPTH, d, D_FF), d ** -0.5)
    inp["ffn2_wu"] = nrm((DEPTH, d, D_FF), d ** -0.5)
    inp["ffn2_wd"] = nrm((DEPTH, D_FF, d), BETA * D_FF ** -0.5)
    inp["ln3_g"] = 1.0 + nrm((DEPTH, d), 0.05)
    inp["ln3_b"] = nrm((DEPTH, d), 0.02)
    return inp


def reference(x_prompt, x_sample, state_wkv, state_shift, state_conv,
              ln1_g, ln1_b, ffn1_wg, ffn1_wu, ffn1_wd, w_in, mu_shift, w0, w_lora_up,
              a0, a_lora_up, g_lora_up, k_k, k_a, r_k, lnx_g, lnx_b, conv_w, w_out,
              ln2_g, ln2_b, ffn2_wg, ffn2_wu, ffn2_wd, ln3_g, ln3_b):
    xp = x_prompt
    xs = x_sample
    bp = x_prompt.shape[0]
    wkv_p, shift_p, conv_p = [], [], []
    wkv_s, shift_s, conv_s = [], [], []
    for l in range(DEPTH):
        p = {
            "ln1_g": ln1_g[l], "ln1_b": ln1_b[l],
            "ffn1_wg": ffn1_wg[l], "ffn1_wu": ffn1_wu[l], "ffn1_wd": ffn1_wd[l],
            "w_in": w_in[l], "mu_shift": mu_shift[l], "w0": w0[l], "w_lora_up": w_lora_up[l],
            "a0": a0[l], "a_lora_up": a_lora_up[l], "g_lora_up": g_lora_up[l],
            "k_k": k_k[l], "k_a": k_a[l], "r_k": r_k[l], "lnx_g": lnx_g[l], "lnx_b": lnx_b[l],
            "conv_w": conv_w[l], "w_out": w_out[l],
            "ln2_g": ln2_g[l], "ln2_b": ln2_b[l],
            "ffn2_wg": ffn2_wg[l], "ffn2_wu": ffn2_wu[l], "ffn2_wd": ffn2_wd[l],
            "ln3_g": ln3_g[l], "ln3_b": ln3_b[l],
        }
        wkv0_p = jnp.zeros((bp, N_HEADS_RWKV, HEAD_SIZE, HEAD_SIZE), jnp.float32)
        shift0_p = jnp.zeros((bp, P_SHIFT), xp.dtype)
        conv0_p = jnp.zeros((bp, CONV_W - 1, D_CONV), xp.dtype)
        xp, wp, sp, cp = trunk_layer(xp, wkv0_p, shift0_p, conv0_p, p)
        xs, wsm, ssm, csm = trunk_layer(xs, state_wkv[l], state_shift[l], state_conv[l], p)
        wkv_p.append(wp); shift_p.append(sp); conv_p.append(cp)
        wkv_s.append(wsm); shift_s.append(ssm); conv_s.append(csm)
    return (xp, xs,
            jnp.stack(wkv_p), jnp.stack(shift_p), jnp.stack(conv_p),
            jnp.stack(wkv_s), jnp.stack(shift_s), jnp.stack(conv_s))
```

```python
import contextlib
import numpy as np
import concourse.bass as bass
import concourse.mybir as mybir
from concourse.bass_utils import run_bass_kernel_spmd

F32 = mybir.dt.float32
BF16 = mybir.dt.bfloat16
F32R = mybir.dt.float32r
AF = mybir.ActivationFunctionType
ALU = mybir.AluOpType
AX = mybir.AxisListType

NCORES = 8
D = 1024
DFF = 2816
NF = DFF // 128
TP = 2048
TS = 64
NB = 16
NTOK = TP + TS
PSH = 1792
PTOT = 3328
ALPHA = 2.0 ** 0.25
LN_EPS = 1e-5
GN_EPS = 64e-5

ENGS = ["pe", "act", "dve", "pool", "sp"]
PROJ_GROUPS = [(0, 4), (4, 8), (8, 12), (12, 14), (18, 22), (22, 26), (14, 18)]
DBG = {}


class Prog:
    def __init__(self, nc, stack):
        self.nc = nc
        self.stack = stack
        self.ops = {e: [] for e in ENGS}
        self.sems = {}
        self.cnt = {}
        self.waited = {e: {} for e in ENGS}
        self.outstanding = []
        self.wtok = {}
        self.rtoks = {}
        self.pend = {e: ([], []) for e in ENGS}

    def _tdeps(self, rd, wr):
        d = []
        for b in rd:
            d.append(self.wtok.get(b))
        for b in wr:
            d.append(self.wtok.get(b))
            d.extend(self.rtoks.get(b, ()))
        return d

    def _tcommit(self, tok, rd, wr):
        for b in rd:
            self.rtoks.setdefault(b, []).append(tok)
        for b in wr:
            self.wtok[b] = tok
            self.rtoks[b] = []

    def barrier(self):
        toks = [self.last(e) for e in ENGS] + list(self.outstanding)
        for e in ENGS:
            wl = self._waits(e, toks)
            if wl:
                self.ops[e].append((None, wl, None))

    def sem(self, key):
        if key not in self.sems:
            self.sems[key] = self.stack.enter_context(self.nc.semaphore("s_" + str(key)))
            self.cnt[key] = 0
        return self.sems[key]

    def _flat(self, deps, acc):
        if deps is None:
            return
        if isinstance(deps, tuple) and len(deps) == 2 and isinstance(deps[1], int) and not isinstance(deps[0], tuple):
            acc.append(deps)
            return
        for d in deps:
            self._flat(d, acc)

    def _waits(self, eng, deps):
        acc = []
        self._flat(deps, acc)
        w = {}
        for k, n in acc:
            if n > w.get(k, 0):
                w[k] = n
        wl = []
        for k, n in w.items():
            if self.waited[eng].get(k, 0) >= n:
                continue
            self.waited[eng][k] = n
            wl.append((k, n))
        return wl

    def op(self, eng, deps, meth, *args, sig=True, rd=(), wr=(), **kwargs):
        fn = (lambda e, m=meth, a=args, k=kwargs: getattr(e, m)(*a, **k))
        rd = list(rd)
        wr = list(wr)
        wl = self._waits(eng, [deps, self._tdeps(rd, wr)])
        tok = None
        inc = None
        if sig:
            self.sem(eng)
            self.cnt[eng] += 1
            tok = (eng, self.cnt[eng])
            inc = (eng, 1)
            prd, pwr = self.pend[eng]
            self._tcommit(tok, prd + rd, pwr + wr)
            self.pend[eng] = ([], [])
        else:
            self.pend[eng][0].extend(rd)
            self.pend[eng][1].extend(wr)
        self.ops[eng].append((fn, wl, inc))
        return tok

    def dma(self, q, chan, out, in_, deps=(), is_out=False, rd=(), wr=()):
        rd = list(rd)
        wr = list(wr)
        wl = self._waits(q, [deps, self._tdeps(rd, wr)])
        self.sem(chan)
        self.cnt[chan] += 16
        tok = (chan, self.cnt[chan])
        self._tcommit(tok, rd, wr)
        self.ops[q].append((lambda e, o=out, i=in_: e.dma_start(out=o, in_=i), wl, (chan, 16)))
        if is_out:
            self.outstanding.append(tok)
        return tok

    def pe_fence(self):
        tok = self.last("pe")
        wl = self._waits("pe", [tok])
        if wl:
            self.ops["pe"].append((None, wl, None))

    def last(self, eng):
        return (eng, self.cnt.get(eng, 0)) if self.cnt.get(eng, 0) > 0 else None

    def flush(self, final=False):
        nc = self.nc
        ops = self.ops
        sems = self.sems
        if final:
            wl = self._waits("sp", self.outstanding)
            ops["sp"].append((None, wl, None))

        def replay(e, lst):
            for fn, wl, inc in lst:
                for k, n in wl:
                    e.wait_ge(sems[k], n)
                if fn is None:
                    continue
                ins = fn(e)
                if inc is not None:
                    ins.then_inc(sems[inc[0]], inc[1])

        with nc.Block() as block:
            @block.tensor
            def _(e):
                replay(e, ops["pe"])

            @block.scalar
            def _(e):
                replay(e, ops["act"])

            @block.vector
            def _(e):
                replay(e, ops["dve"])

            @block.gpsimd
            def _(e):
                replay(e, ops["pool"])

            @block.sync
            def _(e):
                replay(e, ops["sp"])
        self.ops = {e: [] for e in ENGS}

    def barrier_tokens(self):
        toks = [self.last(e) for e in ENGS if e in self.cnt]
        return [t for t in toks if t is not None]


def ffn_phase(P, nc, name, x_src, x_dst, wg_d, wu_d, wd_d, lng_d, lnb_d, supers, start_deps):
    with contextlib.ExitStack() as stk:
        def sb(nm, shape, dt):
            return stk.enter_context(nc.sbuf_tensor(name + nm, shape, dt))

        def ps(nm, shape, dt):
            return stk.enter_context(nc.psum_tensor(name + nm, shape, dt))

        wg = sb("wg", [128, 8, DFF], BF16)
        wu = sb("wu", [128, 8, DFF], BF16)
        wd = sb("wd", [128, NF, D], BF16)
        xbf = sb("xbf", [128, 4, D], BF16)
        xT = sb("xT", [128, 8, 512], BF16)
        hT = sb("hT", [128, NF, 512], BF16)
        ssb = [sb("ssb%d" % i, [128, 512], BF16) for i in range(2)]
        xf = [sb("xf%d" % i, [128, D], F32) for i in range(2)]
        zt = [sb("z%d" % i, [128, D], F32) for i in range(2)]
        gt = sb("lng", [128, D], F32)
        bt = sb("lnb", [128, D], F32)
        ident = sb("ident", [128, 128], BF16)
        identf = sb("identf", [128, 128], F32)
        stats = sb("stats", [128, 2, 6], F32)
        mv = sb("mv", [128, 2], F32)
        rs = sb("rs", [128, 2], F32)
        mhalf = sb("mhalf", [128, 1], F32)
        pg = [ps("pg%d" % i, [128, 512], F32) for i in range(2)]
        pu = [ps("pu%d" % i, [128, 512], F32) for i in range(2)]
        py = [ps("py%d" % i, [128, 512], F32) for i in range(2)]
        ptr = [ps("ptr%d" % i, [128, 1024], BF16) for i in range(2)]

        sd = list(start_deps)
        t_id0 = P.op("pool", sd, "memset", identf[:], 0.0)
        t_id1 = P.op("pool", [t_id0], "affine_select", out=identf[:], in_=identf[:], pattern=[[-1, 128]],
                     compare_op=ALU.not_equal, fill=1.0, base=0, channel_multiplier=1)
        t_ident = P.op("pool", [t_id1], "tensor_copy", ident[:], identf[:])
        t_mh = P.op("pool", sd, "memset", mhalf[:], -0.5)
        t_g = P.dma("sp", name + "c_g", gt[:], lng_d.partition_broadcast(128), sd)
        t_b = P.dma("sp", name + "c_b", bt[:], lnb_d.partition_broadcast(128), sd)

        wgv = wg_d.rearrange("(k p) f -> p k f", p=128)
        wuv = wu_d.rearrange("(k p) f -> p k f", p=128)
        wdv = wd_d.rearrange("(f p) d -> p f d", p=128)
        fpieces = [(0, 2), (2, 6), (6, 14), (14, 22)]
        t_wg = {}
        t_wu = {}
        t_wd = {}

        def load_x(si, deps):
            tok0, ntok = supers[si]
            nt = (ntok + 127) // 128
            pp = min(ntok, 128)
            src = x_src[tok0:tok0 + ntok, :].rearrange("(t p) d -> p t d", p=pp)
            return P.dma("pool", name + "xbf", xbf[:pp, :nt, :], src, deps)

        x_tok = {}
        x_tok[0] = load_x(0, sd)
        for pi, (a, b) in enumerate(fpieces):
            tg = P.dma("pool", name + "wg%d" % pi, wg[:, :, a * 128:b * 128], wgv[:, :, a * 128:b * 128], sd)
            tu = P.dma("pool", name + "wu%d" % pi, wu[:, :, a * 128:b * 128], wuv[:, :, a * 128:b * 128], sd)
            for f in range(a, b):
                t_wg[f] = tg
                t_wu[f] = tu
        for pi, (a, b) in enumerate([(0, 6), (6, 14), (14, 22)]):
            td = P.dma("pool", name + "wd%d" % pi, wd[:, a:b, :], wdv[:, a:b, :], sd)
            for f in range(a, b):
                t_wd[f] = td

        st = dict(xT_free=[], xbf_free=None, stat_free=None, ep_i=0, gi=0, tri=0)
        hT_read = {}
        pg_free = [None, None]
        pu_free = [None, None]
        py_free = [None, None]
        ptr_free = [None, None]
        ssb_free = [None, None]
        xf_free = [None, None]
        z_free = [None, None]

        def transposes(si):
            tok0, ntok = supers[si]
            nt = (ntok + 127) // 128
            pp = min(ntok, 128)
            evs = []
            pe_last = None
            for kp in range(4):
                b = st["tri"] % 2
                st["tri"] += 1
                for kk in range(2):
                    k = kp * 2 + kk
                    for t in range(nt):
                        last = (kk == 1 and t == nt - 1)
                        pe_last = P.op(
                            "pe", [x_tok[si], t_ident, ptr_free[b]] if (kk == 0 and t == 0) else [],
                            "transpose", ptr[b][:, kk * 512 + t * 128: kk * 512 + t * 128 + pp],
                            xbf[:pp, t, k * 128:(k + 1) * 128], ident[:pp, :pp], sig=last)
                ev = P.op(
                    "act", [pe_last] + (st["xT_free"] if kp == 0 else []), "activation",
                    out=xT[:, kp * 2:kp * 2 + 2, :ntok],
                    in_=ptr[b][:, :].rearrange("p (k n) -> p k n", k=2)[:, :, :ntok], func=AF.Copy)
                ptr_free[b] = ev
                evs.append(ev)
            st["xbf_free"] = pe_last
            return evs

        xT_ready = transposes(0)

        for si, (tok0, ntok) in enumerate(supers):
            nt = (ntok + 127) // 128
            pp = min(ntok, 128)
            h_ready = {}
            gu_last = None
            for f in range(NF):
                b = st["gi"] % 2
                st["gi"] += 1
                for k in range(8):
                    tgk = P.op("pe", [t_wg[f], xT_ready, pg_free[b]] if k == 0 else [], "matmul",
                               pg[b][:, :ntok], wg[:, k, f * 128:(f + 1) * 128], xT[:, k, :ntok],
                               start=(k == 0), stop=(k == 7), sig=(k == 7))
                for k in range(8):
                    tuk = P.op("pe", [t_wu[f], pu_free[b]] if k == 0 else [], "matmul",
                               pu[b][:, :ntok], wu[:, k, f * 128:(f + 1) * 128], xT[:, k, :ntok],
                               start=(k == 0), stop=(k == 7), sig=(k == 7))
                gu_last = tuk
                ts_ = P.op("act", [tgk, ssb_free[b]], "activation",
                           out=ssb[b][:, :ntok], in_=pg[b][:, :ntok], func=AF.Silu)
                pg_free[b] = ts_
                th = P.op("dve", [ts_, tuk, hT_read.get(f)], "scalar_tensor_tensor",
                          out=hT[:, f, :ntok], in0=ssb[b][:, :ntok], scalar=0.5, in1=pu[b][:, :ntok],
                          op0=ALU.mult, op1=ALU.mult)
                ssb_free[b] = th
                pu_free[b] = th
                h_ready[f] = th
            st["xT_free"] = [gu_last]
            if si + 1 < len(supers):
                x_tok[si + 1] = load_x(si + 1, [st["xbf_free"]])
            for t in range(nt):
                xb = st["ep_i"] % 2
                zb = xb
                st["ep_i"] += 1
                txf = P.dma("sp", name + "xf%d" % xb, xf[xb][:pp, :],
                            x_src[tok0 + t * 128: tok0 + t * 128 + pp, :], [xf_free[xb]])
                ylast = []
                for half in range(2):
                    for f in range(NF):
                        ty = P.op("pe", [h_ready[f], t_wd[f]] + ([py_free[half]] if f == 0 else []), "matmul",
                                  py[half][:pp, :], hT[:, f, t * 128:t * 128 + pp],
                                  wd[:, f, half * 512:(half + 1) * 512],
                                  start=(f == 0), stop=(f == NF - 1), sig=(f == NF - 1))
                    ylast.append(ty)
                if t == nt - 1:
                    for f in range(NF):
                        hT_read[f] = ylast[1]
                z = zt[zb]
                tz = []
                for half in range(2):
                    tzz = P.op("dve", [ylast[half], txf, z_free[zb]], "scalar_tensor_tensor",
                               out=z[:pp, half * 512:(half + 1) * 512],
                               in0=xf[xb][:pp, half * 512:(half + 1) * 512],
                               scalar=ALPHA, in1=py[half][:pp, :], op0=ALU.mult, op1=ALU.add)
                    py_free[half] = tzz
                    tz.append(tzz)
                xf_free[xb] = tz[1]
                tst = []
                for half in range(2):
                    tst.append(P.op("dve", [tz[half], st["stat_free"]], "bn_stats",
                                    out=stats[:pp, half, :], in_=z[:pp, half * 512:(half + 1) * 512]))
                tag = P.op("dve", tst, "bn_aggr", out=mv[:pp, :],
                           in_=stats[:pp, :, :].rearrange("p a b -> p (a b)"))
                t1 = P.op("pool", [tag, t_mh], "tensor_scalar", out=rs[:pp, 0:1], in0=mv[:pp, 1:2],
                          scalar1=LN_EPS, scalar2=None, op0=ALU.add)
                t2 = P.op("pool", [t1], "tensor_tensor", out=rs[:pp, 1:2], in0=rs[:pp, 0:1],
                          in1=mhalf[:pp, 0:1], op=ALU.pow)
                tn = P.op("dve", [t2, tag], "tensor_scalar", out=z[:pp, :], in0=z[:pp, :],
                          scalar1=mv[:pp, 0:1], scalar2=rs[:pp, 1:2], op0=ALU.subtract, op1=ALU.mult)
                st["stat_free"] = tn
                tg1 = P.op("pool", [tn, t_g], "tensor_tensor", out=z[:pp, :], in0=z[:pp, :],
                           in1=gt[:pp, :], op=ALU.mult)
                tg2 = P.op("pool", [tg1, t_b], "tensor_tensor", out=z[:pp, :], in0=z[:pp, :],
                           in1=bt[:pp, :], op=ALU.add)
                tout = P.dma("sp", name + "zo%d" % zb, x_dst[tok0 + t * 128: tok0 + t * 128 + pp, :],
                             z[:pp, :], [tg2], is_out=True)
                z_free[zb] = tout
                if t == 0 and si + 1 < len(supers):
                    xT_ready = transposes(si + 1)
        P.flush()


class NS:
    pass


class Rot:
    def __init__(self, banks):
        self.banks = banks
        self.i = 0

    def get(self):
        b, key = self.banks[self.i % len(self.banks)]
        self.i += 1
        return b, key


def rsqrt_pool(P, n, y, x, t, mh, ky, kx, kt):
    P.op("pool", [], "tensor_tensor", out=y, in0=x, in1=mh, op=ALU.pow, rd=[kx, "mh"], wr=[ky])
    P.op("pool", [], "tensor_tensor", out=t, in0=x, in1=y, op=ALU.mult, rd=[kx, ky], wr=[kt])
    P.op("pool", [], "tensor_tensor", out=t, in0=t, in1=y, op=ALU.mult, rd=[kt, ky], wr=[kt])
    P.op("pool", [], "tensor_scalar", out=t, in0=t, scalar1=-0.5, scalar2=1.5, op0=ALU.mult, op1=ALU.add,
         rd=[kt], wr=[kt])
    P.op("pool", [], "tensor_tensor", out=y, in0=y, in1=t, op=ALU.mult, rd=[ky, kt], wr=[ky])


def ln_epilogue(P, B, pp, ybanks, x_res_d, x_dst_d, g_t, b_t, kg, kb, pfx):
    zk = [pfx + "z0", pfx + "z1"]
    if x_res_d is not None:
        P.dma("sp", pfx + "zld", B.z[:pp, :], x_res_d, wr=zk)
    for half in range(2):
        yb, ykey = ybanks[half]
        P.op("dve", [], "scalar_tensor_tensor", out=B.z[:pp, half * 512:(half + 1) * 512],
             in0=B.z[:pp, half * 512:(half + 1) * 512], scalar=ALPHA, in1=yb[:pp, :],
             op0=ALU.mult, op1=ALU.add, rd=[zk[half], ykey], wr=[zk[half]])
    yield
    for half in range(2):
        P.op("dve", [], "bn_stats", out=B.stats[:pp, half, :], in_=B.z[:pp, half * 512:(half + 1) * 512],
             rd=[zk[half]], wr=[pfx + "st%d" % half])
    P.op("dve", [], "bn_aggr", out=B.mv[:pp, :], in_=B.stats[:pp, :, :].rearrange("p a b -> p (a b)"),
         rd=[pfx + "st0", pfx + "st1"], wr=[pfx + "mv"])
    yield
    P.op("pool", [], "tensor_scalar", out=B.rs[:pp, 0:1], in0=B.mv[:pp, 1:2], scalar1=LN_EPS, scalar2=None,
         op0=ALU.add, rd=[pfx + "mv"], wr=[pfx + "rs0"])
    rsqrt_pool(P, 1, B.rs[:pp, 1:2], B.rs[:pp, 0:1], B.rs[:pp, 2:3], B.mh1[:pp, 0:1],
               pfx + "rs1", pfx + "rs0", pfx + "rs2")
    yield
    P.op("dve", [], "tensor_scalar", out=B.z[:pp, :], in0=B.z[:pp, :], scalar1=B.mv[:pp, 0:1],
         scalar2=B.rs[:pp, 1:2], op0=ALU.subtract, op1=ALU.mult,
         rd=[pfx + "mv", pfx + "rs1"] + zk, wr=zk)
    yield
    P.op("pool", [], "tensor_tensor", out=B.z[:pp, :], in0=B.z[:pp, :], in1=g_t[:pp, :], op=ALU.mult,
         rd=zk + [kg], wr=zk)
    P.op("pool", [], "tensor_tensor", out=B.z[:pp, :], in0=B.z[:pp, :], in1=b_t[:pp, :], op=ALU.add,
         rd=zk + [kb], wr=zk)
    yield
    P.dma("sp", pfx + "zo", x_dst_d, B.z[:pp, :], is_out=True, rd=zk)


def mixer_phase(P, nc, x1_d, x2_d, W, ST, OUT, SCR):
    c_dec = float(np.exp(-0.5))
    P.barrier()
    with contextlib.ExitStack() as stk0:
        def sb0(nm, shape, dt=F32):
            return stk0.enter_context(nc.sbuf_tensor("m_" + nm, shape, dt))

        w_in = sb0("w_in", [128, 8, PTOT], BF16)
        w_out = sb0("w_out", [128, 8, D], BF16)
        wlu = sb0("wlu", [128, 512])
        alu = sb0("alu", [128, 512])
        glu = sb0("glu", [128, 512])
        kkb = sb0("kkb", [128, 512])
        kab = sb0("kab", [128, 512])
        rkb = sb0("rkb", [128, 512])
        lgb = sb0("lgb", [128, 512])
        lbb = sb0("lbb", [128, 512])
        g2t = sb0("g2t", [128, D])
        b2t = sb0("b2t", [128, D])
        rows = sb0("rows", [1, 1152])
        identf = sb0("identf", [128, 128])
        identb = sb0("identb", [128, 128], BF16)
        M_si = sb0("M_si", [128, 256])
        M_L = sb0("M_L", [128, 128])
        triI = sb0("triI", [128, 128])
        triS = sb0("triS", [128, 128])
        negc = sb0("negc", [128, 2])
        muF = sb0("muF", [128, 14])
        cwF = sb0("cwF", [128, 12])
        mh = sb0("mh", [128, 8])
        small_a = sb0("small_a", [16, 128])
        small_b = sb0("small_b", [16, 128])

        wiv = W["w_in"].rearrange("(k p) n -> p k n", p=128)
        for gi_, (m0_, m1_) in enumerate(PROJ_GROUPS):
            P.dma("pool", "c_win_g%d" % gi_, w_in[:, :, m0_ * 128:m1_ * 128], wiv[:, :, m0_ * 128:m1_ * 128],
                  wr=["w_in_g%d" % gi_])
        P.dma("pool", "c_wout", w_out[:], W["w_out"].rearrange("(k p) n -> p k n", p=128), wr=["w_out"])
        P.op("pool", [], "memset", identf[:], 0.0, wr=["identf"])
        P.op("pool", [], "affine_select", out=identf[:], in_=identf[:], pattern=[[-1, 128]],
             compare_op=ALU.not_equal, fill=1.0, base=0, channel_multiplier=1, rd=["identf"], wr=["identf"])
        P.op("pool", [], "tensor_copy", identb[:], identf[:], rd=["identf"], wr=["identb"])
        P.op("pool", [], "memset", M_si[:], 1.0, wr=["M_si"])
        P.op("pool", [], "memset", M_L[:], 1.0, wr=["M_L"])
        for j in range(2):
            P.op("pool", [], "affine_select", out=M_si[:, j * 128:(j + 1) * 128], in_=M_si[:, j * 128:(j + 1) * 128],
                 pattern=[[1, 128]], compare_op=(ALU.is_gt if j % 2 == 0 else ALU.is_ge), fill=0.0, base=0,
                 channel_multiplier=-1, rd=["M_si"], wr=["M_si"])
        P.op("pool", [], "affine_select", out=M_L[:, :], in_=M_L[:, :],
             pattern=[[-1, 128]], compare_op=ALU.is_gt, fill=0.0, base=0,
             channel_multiplier=1, rd=["M_L"], wr=["M_L"])
        P.op("pool", [], "memset", triI[:], -c_dec, wr=["triI"])
        P.op("pool", [], "affine_select", out=triI[:], in_=triI[:], pattern=[[1, 128]], compare_op=ALU.is_ge,
             fill=0.0, base=0, channel_multiplier=-1, rd=["triI"], wr=["triI"])
        P.op("pool", [], "memset", triS[:], -c_dec, wr=["triS"])
        P.op("pool", [], "affine_select", out=triS[:], in_=triS[:], pattern=[[1, 128]], compare_op=ALU.is_gt,
             fill=0.0, base=0, channel_multiplier=-1, rd=["triS"], wr=["triS"])
        P.op("pool", [], "memset", negc[:], -c_dec, wr=["negc"])
        P.op("pool", [], "memset", mh[:], -0.5, wr=["mh"])
        P.op("pool", [], "memset", rows[0:1, 0:128], 1.0, wr=["rows_ones"])
        P.dma("sp", "c_w0", rows[0:1, 128:640], W["w0"], wr=["rows_w0"])
        P.dma("sp", "c_a0", rows[0:1, 640:1152], W["a0"], wr=["rows_a0"])
        P.dma("sp", "c_wlu", wlu[0:64, :], W["w_lora_up"], wr=["wlu"])
        P.dma("sp", "c_alu", alu[64:128, :], W["a_lora_up"], wr=["alu"])
        P.dma("sp", "c_glu", glu[:, :], W["g_lora_up"], wr=["glu"])
        for nm, t_, src in [("kkb", kkb, "k_k"), ("kab", kab, "k_a"), ("rkb", rkb, "r_k"), ("lgb", lgb, "lnx_g"),
                            ("lbb", lbb, "lnx_b"), ("g2t", g2t, "ln2_g"), ("b2t", b2t, "ln2_b")]:
            P.dma("sp", "c_" + nm, t_[:], W[src].partition_broadcast(128), wr=[nm])
        P.dma("sp", "c_mu", small_a[0:14, :], W["mu_shift"].rearrange("o (m p) -> (o m) p", p=128), wr=["small_a"])
        P.dma("sp", "c_cw", small_b[0:12, :], W["conv_w"].rearrange("w (c p) -> (w c) p", p=128), wr=["small_b"])

        def alloc_common(stk, B, pfx):
            def sb(nm, shape, dt=F32):
                t_ = stk.enter_context(nc.sbuf_tensor("mx" + pfx + "_" + nm, shape, dt))
                setattr(B, nm, t_)
                return t_
            sb("xbf", [128, D], BF16)
            sb("xT", [128, 8, 128], BF16)
            sb("psT", [128, 14, 144])
            sb("carry", [128, 14, 16])
            sb("cc_sb", [128, 4, 128])
            sb("u", [128, 4, 160])
            sb("zc", [128, 4, 128])
            sb("ucarry", [128, 4, 32])
            sb("dtmp", [128, 128])
            sb("rkvb", [128, 12, 128], BF16)
            sb("tanhwd", [128, 128])
            sb("siggd", [128, 128])
            for nm in ["r", "k", "v0", "v1", "sigw", "alr", "kkn", "tmp", "k2", "bb", "O", "g_sb0", "g_sb1"]:
                sb(nm, [128, 512])
            for nm in ["ss", "inv", "nt", "nt2", "bonus0", "bonus1", "s1", "s2", "mean", "rstd"]:
                sb(nm, [128, 8])
            sb("yrw", [128, 512], BF16)
            sb("ymixT0", [128, 8, 128], BF16)
            sb("ymixT1", [128, 8, 128], BF16)
            sb("z", [128, D])
            sb("stats", [128, 2, 6])
            sb("mv", [128, 2])
            sb("rs", [128, 3])
            B.mh1 = mh
            return sb

        def tr_consts(rot):
            bk, key = rot.get()
            P.op("pe", [], "transpose", bk[:, 0:14], small_a[0:14, :], identf[0:14, 0:14],
                 rd=["small_a", "identf"], wr=[key])
            P.op("act", [], "activation", out=muF[:, :], in_=bk[:, 0:14], func=AF.Copy, rd=[key], wr=["muF"])
            bk, key = rot.get()
            P.op("pe", [], "transpose", bk[:, 0:12], small_b[0:12, :], identf[0:12, 0:12],
                 rd=["small_b", "identf"], wr=[key])
            P.op("act", [], "activation", out=cwF[:, :], in_=bk[:, 0:12], func=AF.Copy, rd=[key], wr=["cwF"])
            P.pe_fence()

        def front(B, rot, tok0, ntok, shift, first, par):
            s2 = 2 * shift
            Bv = getattr(B, "v%d" % par)
            Bg = getattr(B, "g_sb%d" % par)
            Bbon = getattr(B, "bonus%d" % par)
            Bym = getattr(B, "ymixT%d" % par)
            kv, kg, kbon, kymc = "v%d" % par, "g_sb%d" % par, "bonus%d" % par, "ymixT_c%d" % par
            psk = ["ps%d" % m for m in range(14)]
            if first is True:
                P.op("pool", [], "memset", B.psT[:, :, 0:shift], 0.0, wr=psk)
                P.op("pool", [], "memset", B.u[:, :, 0:s2], 0.0, wr=["u"])
            elif first is False:
                P.op("pool", [], "tensor_copy", B.psT[:, :, 0:shift], B.carry[:, :, 0:shift], rd=["carry"], wr=psk)
                P.op("pool", [], "tensor_copy", B.u[:, :, 0:s2], B.ucarry[:, :, 0:s2], rd=["ucarry"], wr=["u"])
            P.dma("pool", "ld_xbf", B.xbf[:ntok, :], x1_d[tok0:tok0 + ntok, :], wr=["xbf"])
            bk, key = rot.get()
            bkb = bk[:, :].bitcast(BF16)
            for k in range(8):
                P.op("pe", [], "transpose", bkb[:, k * 128:k * 128 + ntok], B.xbf[:ntok, k * 128:(k + 1) * 128],
                     identb[:ntok, :ntok], rd=["xbf", "identb"], wr=[key], sig=(k == 7))
            P.op("act", [], "activation", out=B.xT[:, :, :ntok],
                 in_=bkb.rearrange("p (k n) -> p k n", k=8)[:, :, :ntok], func=AF.Copy, rd=[key], wr=["xT"])
            yield
            groups = [(0, 4, "sh"), (4, 8, "sh"), (8, 12, "sh"), (12, 14, "sh"), (18, 22, "cc"), (22, 26, "ch"),
                      (14, 18, "cb")]
            cb_view = None
            cb_key = None
            for gi, (m0, m1, kind) in enumerate(groups):
                bk, key = rot.get()
                for m in range(m0, m1):
                    for k in range(8):
                        P.op("pe", [], "matmul", bk[:, (m - m0) * 128:(m - m0) * 128 + ntok],
                             w_in[:, k, m * 128:(m + 1) * 128], B.xT[:, k, :ntok], start=(k == 0), stop=(k == 7),
                             rd=["w_in_g%d" % gi, "xT"], wr=[key], sig=(m == m1 - 1 and k == 7))
                view = bk[:, :].rearrange("p (m n) -> p m n", n=128)[:, :m1 - m0, :ntok]
                if kind == "sh":
                    dst = B.psT[:, m0:m1, shift:shift + ntok]
                    if gi % 2 == 0:
                        P.op("act", [], "activation", out=dst, in_=view, func=AF.Copy, rd=[key], wr=psk[m0:m1])
                    else:
                        P.op("dve", [], "tensor_copy", dst, view, rd=[key], wr=psk[m0:m1])
                elif kind == "cc":
                    P.op("act", [], "activation", out=B.cc_sb[:, :, :ntok], in_=view, func=AF.Copy,
                         rd=[key], wr=["cc_sb"])
                elif kind == "ch":
                    P.op("dve", [], "tensor_tensor", out=B.u[:, :, s2:s2 + ntok], in0=B.cc_sb[:, :, :ntok], in1=view,
                         op=ALU.mult, rd=["cc_sb", key], wr=["u"])
                else:
                    cb_view = view
                    cb_key = key
                yield
            P.op("pool", [], "tensor_copy", B.carry[:, :, 0:shift], B.psT[:, :, ntok:ntok + shift],
                 rd=psk, wr=["carry"])
            for c in range(4):
                P.op("dve", [], "tensor_scalar", out=B.zc[:, c, :ntok], in0=B.u[:, c, 0:ntok],
                     scalar1=cwF[:, c:c + 1], scalar2=None, op0=ALU.mult, rd=["u", "cwF"], wr=["zc%d" % c])
                for w_ in (1, 2):
                    P.op("dve", [], "scalar_tensor_tensor", out=B.zc[:, c, :ntok],
                         in0=B.u[:, c, w_ * shift:w_ * shift + ntok], scalar=cwF[:, w_ * 4 + c:w_ * 4 + c + 1],
                         in1=B.zc[:, c, :ntok], op0=ALU.mult, op1=ALU.add, rd=["u", "cwF", "zc%d" % c],
                         wr=["zc%d" % c])
                yield
            P.op("dve", [], "tensor_tensor", out=Bym[:, 4:8, :ntok], in0=B.zc[:, :, :ntok], in1=cb_view,
                 op=ALU.mult, rd=["zc0", "zc1", "zc2", "zc3", cb_key], wr=[kymc])
            P.op("pool", [], "tensor_copy", B.ucarry[:, :, 0:s2], B.u[:, :, ntok:ntok + s2], rd=["u"], wr=["ucarry"])
            for m in range(14):
                P.op("dve", [], "tensor_tensor", out=B.dtmp[:, :ntok], in0=B.psT[:, m, 0:ntok],
                     in1=B.psT[:, m, shift:shift + ntok], op=ALU.subtract, rd=[psk[m]], wr=["dtmp"])
                P.op("dve", [], "scalar_tensor_tensor", out=B.psT[:, m, shift:shift + ntok], in0=B.dtmp[:, :ntok],
                     scalar=muF[:, m:m + 1], in1=B.psT[:, m, shift:shift + ntok], op0=ALU.mult, op1=ALU.add,
                     rd=["dtmp", "muF", psk[m]], wr=[psk[m]])
                if m % 3 == 2:
                    yield
            P.op("act", [], "activation", out=B.tanhwd[0:64, :ntok], in_=B.psT[0:64, 12, shift:shift + ntok],
                 func=AF.Tanh, rd=[psk[12]], wr=["tanhwd"])
            P.op("act", [], "activation", out=B.siggd[:, :ntok], in_=B.psT[:, 13, shift:shift + ntok],
                 func=AF.Sigmoid, rd=[psk[13]], wr=["siggd"])
            bkw, keyw = rot.get()
            P.op("pe", [], "matmul", bkw[:ntok, :], rows[0:1, 0:ntok], rows[0:1, 128:640], start=True, stop=False,
                 rd=["rows_ones", "rows_w0"], wr=[keyw], sig=False)
            P.op("pe", [], "matmul", bkw[:ntok, :], B.tanhwd[0:64, :ntok], wlu[0:64, :], start=False, stop=True,
                 rd=["tanhwd", "wlu"], wr=[keyw])
            P.op("act", [], "activation", out=B.sigw[:ntok, :], in_=bkw[:ntok, :], func=AF.Sigmoid,
                 rd=[keyw], wr=["sigw"])
            bka, keya = rot.get()
            P.op("pe", [], "matmul", bka[:ntok, :], rows[0:1, 0:ntok], rows[0:1, 640:1152], start=True, stop=False,
                 rd=["rows_ones", "rows_a0"], wr=[keya], sig=False)
            P.op("pe", [], "matmul", bka[:ntok, :], B.psT[64:128, 12, shift:shift + ntok], alu[64:128, :],
                 start=False, stop=True, rd=[psk[12], "alu"], wr=[keya])
            P.op("act", [], "activation", out=B.alr[:ntok, :], in_=bka[:ntok, :], func=AF.Sigmoid,
                 rd=[keya], wr=["alr"])
            bkg, keyg = rot.get()
            P.op("pe", [], "matmul", bkg[:ntok, :], B.siggd[:, :ntok], glu[:, :], start=True, stop=True,
                 rd=["siggd", "glu"], wr=[keyg])
            P.pe_fence()
            P.op("act", [], "activation", out=Bg[:ntok, :], in_=bkg[:ntok, :], func=AF.Copy,
                 rd=[keyg], wr=[kg])
            yield
            P.op("act", [], "activation", out=B.rkvb[:, :, :ntok], in_=B.psT[:, 0:12, shift:shift + ntok],
                 func=AF.Copy, rd=psk[0:12], wr=["rkvb"])
            for i_, (nm, m0) in enumerate([("r", 0), ("k", 4), ("v", 8)]):
                bk, key = rot.get()
                bkb = bk[:, :].bitcast(BF16)
                for j in range(4):
                    P.op("pe", [], "transpose", bkb[:ntok, j * 128:(j + 1) * 128],
                         B.rkvb[:, m0 + j, :ntok], identb[:, :], rd=["rkvb", "identb"], wr=[key],
                         sig=(j == 3))
                dst = Bv if nm == "v" else getattr(B, nm)
                dk = kv if nm == "v" else nm
                if i_ % 2 == 0:
                    P.op("act", [], "activation", out=dst[:ntok, :], in_=bkb[:ntok, 0:512], func=AF.Copy,
                         rd=[key], wr=[dk])
                else:
                    P.op("dve", [], "tensor_copy", dst[:ntok, :], bkb[:ntok, 0:512], rd=[key], wr=[dk])
                yield
            n = ntok

            def v3(t_):
                return t_[:n, :].rearrange("p (h j) -> p h j", h=8)

            def b3(t_):
                return t_[:n, :].unsqueeze(2).to_broadcast([n, 8, 64])

            P.op("dve", [], "tensor_tensor", out=B.kkn[:n, :], in0=B.k[:n, :], in1=kkb[:n, :], op=ALU.mult,
                 rd=["k", "kkb"], wr=["kkn"])
            P.op("dve", [], "tensor_tensor", out=B.tmp[:n, :], in0=B.kkn[:n, :], in1=B.kkn[:n, :], op=ALU.mult,
                 rd=["kkn"], wr=["tmp"])
            P.op("dve", [], "tensor_reduce", out=B.ss[:n, :], in_=v3(B.tmp), axis=AX.X, op=ALU.add,
                 rd=["tmp"], wr=["ss"])
            P.op("dve", [], "tensor_scalar", out=B.ss[:n, :], in0=B.ss[:n, :], scalar1=1e-24, scalar2=None,
                 op0=ALU.max, rd=["ss"], wr=["ss"])
            yield
            rsqrt_pool(P, 8, B.inv[:n, :], B.ss[:n, :], B.nt[:n, :], mh[:n, :], "inv", "ss", "nt")
            P.op("dve", [], "tensor_tensor", out=v3(B.kkn), in0=v3(B.kkn), in1=b3(B.inv), op=ALU.mult,
                 rd=["kkn", "inv"], wr=["kkn"])
            P.op("dve", [], "scalar_tensor_tensor", out=B.tmp[:n, :], in0=B.alr[:n, :], scalar=-1.0,
                 in1=kab[:n, :], op0=ALU.add, op1=ALU.mult, rd=["alr", "kab"], wr=["tmp"])
            P.op("dve", [], "scalar_tensor_tensor", out=B.k2[:n, :], in0=B.tmp[:n, :], scalar=1.0,
                 in1=B.k[:n, :], op0=ALU.add, op1=ALU.mult, rd=["tmp", "k"], wr=["k2"])
            yield
            P.op("dve", [], "tensor_tensor", out=B.bb[:n, :], in0=B.kkn[:n, :], in1=B.alr[:n, :], op=ALU.mult,
                 rd=["kkn", "alr"], wr=["bb"])
            P.op("dve", [], "tensor_tensor", out=B.tmp[:n, :], in0=B.r[:n, :], in1=B.k2[:n, :], op=ALU.mult,
                 rd=["r", "k2"], wr=["tmp"])
            P.op("dve", [], "tensor_tensor", out=B.tmp[:n, :], in0=B.tmp[:n, :], in1=rkb[:n, :], op=ALU.mult,
                 rd=["tmp", "rkb"], wr=["tmp"])
            P.op("dve", [], "tensor_reduce", out=Bbon[:n, :], in_=v3(B.tmp), axis=AX.X, op=ALU.add,
                 rd=["tmp"], wr=[kbon])
            yield

        def back(B, rot, tok0, ntok, par):
            n = ntok
            Bv = getattr(B, "v%d" % par)
            Bg = getattr(B, "g_sb%d" % par)
            Bbon = getattr(B, "bonus%d" % par)
            Bym = getattr(B, "ymixT%d" % par)
            kv, kg, kbon, kymc, kymr = "v%d" % par, "g_sb%d" % par, "bonus%d" % par, "ymixT_c%d" % par, "ymixT_r%d" % par
            scr = B.z[:, 0:512]
            zk = ["mz0", "mz1"]

            def v3(t_):
                return t_[:n, :].rearrange("p (h j) -> p h j", h=8)

            def b3(t_):
                return t_[:n, :].unsqueeze(2).to_broadcast([n, 8, 64])

            P.op("dve", [], "tensor_reduce", out=B.s1[:n, :], in_=v3(B.O), axis=AX.X, op=ALU.add,
                 rd=["O"], wr=["s1"])
            P.op("dve", [], "tensor_tensor", out=scr[:n, :], in0=B.O[:n, :], in1=B.O[:n, :], op=ALU.mult,
                 rd=["O"], wr=zk)
            P.op("dve", [], "tensor_reduce", out=B.s2[:n, :], in_=v3(scr), axis=AX.X, op=ALU.add,
                 rd=zk, wr=["s2"])
            P.op("dve", [], "tensor_scalar", out=B.mean[:n, :], in0=B.s1[:n, :], scalar1=1.0 / 64, scalar2=None,
                 op0=ALU.mult, rd=["s1"], wr=["mean"])
            P.op("dve", [], "tensor_tensor", out=B.s1[:n, :], in0=B.mean[:n, :], in1=B.mean[:n, :], op=ALU.mult,
                 rd=["mean"], wr=["s1"])
            P.op("dve", [], "scalar_tensor_tensor", out=B.s2[:n, :], in0=B.s2[:n, :], scalar=1.0 / 64,
                 in1=B.s1[:n, :], op0=ALU.mult, op1=ALU.subtract, rd=["s2", "s1"], wr=["s2"])
            P.op("dve", [], "tensor_scalar", out=B.s2[:n, :], in0=B.s2[:n, :], scalar1=GN_EPS, scalar2=None,
                 op0=ALU.add, rd=["s2"], wr=["s2"])
            yield
            rsqrt_pool(P, 8, B.rstd[:n, :], B.s2[:n, :], B.nt2[:n, :], mh[:n, :], "rstd", "s2", "nt2")
            yield
            P.op("dve", [], "tensor_tensor", out=v3(B.O), in0=v3(B.O), in1=b3(B.mean), op=ALU.subtract,
                 rd=["O", "mean"], wr=["O"])
            P.op("dve", [], "tensor_tensor", out=v3(B.O), in0=v3(B.O), in1=b3(B.rstd), op=ALU.mult,
                 rd=["O", "rstd"], wr=["O"])
            P.op("pool", [], "tensor_tensor", out=B.O[:n, :], in0=B.O[:n, :], in1=lgb[:n, :], op=ALU.mult,
                 rd=["O", "lgb"], wr=["O"])
            P.op("pool", [], "tensor_tensor", out=B.O[:n, :], in0=B.O[:n, :], in1=lbb[:n, :], op=ALU.add,
                 rd=["O", "lbb"], wr=["O"])
            yield
            P.op("dve", [], "tensor_tensor", out=v3(scr), in0=v3(Bv), in1=b3(Bbon), op=ALU.mult,
                 rd=[kv, kbon], wr=zk)
            P.op("dve", [], "tensor_tensor", out=B.O[:n, :], in0=B.O[:n, :], in1=scr[:n, :], op=ALU.add,
                 rd=["O"] + zk, wr=["O"])
            P.op("dve", [], "tensor_tensor", out=B.yrw[:n, :], in0=B.O[:n, :], in1=Bg[:n, :], op=ALU.mult,
                 rd=["O", kg], wr=["yrw"])
            yield
            bk, key = rot.get()
            bkb = bk[:, :].bitcast(BF16)
            for q in range(4):
                P.op("pe", [], "transpose", bkb[:, q * 128:q * 128 + n], B.yrw[:n, q * 128:(q + 1) * 128],
                     identb[:n, :n], rd=["yrw", "identb"], wr=[key], sig=(q == 3))
            P.op("act", [], "activation", out=Bym[:, 0:4, :n],
                 in_=bkb[:, 0:512].rearrange("p (q t) -> p q t", q=4)[:, :, :n], func=AF.Copy,
                 rd=[key], wr=[kymr])
            yield
            P.dma("sp", "mzld", B.z[:n, :], x1_d[tok0:tok0 + n, :], wr=zk)
            yield
            ybanks = []
            for half in range(2):
                bk, key = rot.get()
                for k in range(8):
                    P.op("pe", [], "matmul", bk[:n, :], Bym[:, k, :n], w_out[:, k, half * 512:(half + 1) * 512],
                         start=(k == 0), stop=(k == 7), rd=[kymr, kymc, "w_out"], wr=[key], sig=(k == 7))
                ybanks.append((bk, key))
            yield from ln_epilogue(P, B, n, ybanks, None, x2_d[tok0:tok0 + n, :], g2t, b2t, "g2t", "b2t", "m")

        def finals(B, rot, shift, sh_dst, cv_dst):
            s2 = 2 * shift
            for gi, g0 in enumerate(range(0, 14, 4)):
                g1 = min(g0 + 4, 14)
                bk, key = rot.get()
                for m in range(g0, g1):
                    P.op("pe", [], "transpose", bk[:shift, (m - g0) * 128:(m - g0 + 1) * 128], B.carry[:, m, 0:shift],
                         identf[:, :], rd=["carry", "identf"], wr=[key], sig=(m == g1 - 1))
                P.op("act", [], "activation", out=B.tmp[:shift, 0:(g1 - g0) * 128],
                     in_=bk[:shift, 0:(g1 - g0) * 128], func=AF.Copy, rd=[key], wr=["tmp"])
                P.dma("sp", "st_fin_sh%d" % gi, sh_dst[:, g0 * 128:g1 * 128], B.tmp[:shift, 0:(g1 - g0) * 128],
                      is_out=True, rd=["tmp"])
            bk, key = rot.get()
            for c in range(4):
                P.op("pe", [], "transpose", bk[:s2, c * 128:(c + 1) * 128], B.ucarry[:, c, 0:s2], identf[:, :],
                     rd=["ucarry", "identf"], wr=[key], sig=(c == 3))
            P.op("act", [], "activation", out=B.tmp[:s2, :], in_=bk[:s2, :], func=AF.Copy, rd=[key], wr=["tmp"])
            P.dma("sp", "st_fin_cv", cv_dst, B.tmp[:s2, :], is_out=True, rd=["tmp"])

        with contextlib.ExitStack() as stk:
            B = NS()
            sb = alloc_common(stk, B, "a")
            for nm in ["Pm", "Pinv", "Pex"]:
                sb(nm, [128, 512])
            for nm in ["Rtb", "Atb", "Btb", "Kt", "vr"]:
                sb(nm, [128, 512], BF16)
            sb("BT", [128, 4, 128], BF16)
            sb("KT", [128, 4, 128], BF16)
            sb("ARTd", [128, 4, 2, 2, 128], BF16)
            for G in range(2):
                sb("SA%d" % G, [128, 4, 2, 128], BF16)
                sb("SB%d" % G, [128, 4, 2, 128], BF16)
                sb("Xb%d" % G, [128, 4, 128], BF16)
                sb("XM%d" % G, [128, 4, 2, 128], BF16)
            sb("Zs", [128, 8, 64], BF16)
            sb("Us", [128, 8, 64], BF16)
            sb("Hm", [128, 4, 64])
            sb("Hr", [128, 4, 64], BF16)
            sb("pc", [128, 4])
            sb("ht", [128, 4, 64])
            banks = [(stk.enter_context(nc.psum_tensor("mp_b%d" % i, [128, 512], F32)), "PB%d" % i) for i in range(2)]
            rot = Rot(banks[0:2])
            Vg = [stk.enter_context(nc.psum_tensor("mp_v%d" % G, [128, 1536], F32)) for G in range(2)]
            V = [[(Vg[G][:, i * 512:(i + 1) * 512], "V%d_%d" % (G, i)) for i in range(3)] for G in range(2)]
            tr_consts(rot)
            P.op("pool", [], "memset", B.Hm[:], 0.0, wr=["Hm"])
            P.op("pool", [], "memset", B.Hr[:], 0.0, wr=["Hr"])
            P.op("pool", [], "memset", B.ARTd[:], 0.0, wr=["AT", "RT"])

            def R(ap):
                return ap

            def pump(g, n_):
                if g is None:
                    return
                for _ in range(n_):
                    try:
                        next(g)
                    except StopIteration:
                        return

            def drain(g):
                if g is not None:
                    for _ in g:
                        pass

            nch = DBG.get("nchunks", TP // 128)
            drain(front(B, rot, 0, 128, 1, True, 0))
            bprev = None
            for c in range(nch):
                par = c % 2
                Bv = getattr(B, "v%d" % par)
                kv = "v%d" % par
                nxt = front(B, rot, (c + 1) * 128, 128, 1, False, (c + 1) % 2) if c + 1 < nch else None
                if DBG.get("front_only"):
                    drain(nxt)
                    continue
                pump(bprev, 2)
                bk, key = rot.get()
                P.op("pe", [], "matmul", bk[:, :], triI[:, :], B.sigw[:, :], start=True, stop=True,
                     rd=["triI", "sigw"], wr=[key])
                P.pe_fence()
                P.op("act", [], "activation", out=B.Pm[:, :], in_=bk[:, :], func=AF.Exp, rd=[key], wr=["Pm"])
                P.op("act", [], "activation", out=B.Pinv[:, :], in_=bk[:, :], func=AF.Exp, scale=-1.0,
                     rd=[key], wr=["Pinv"])
                bk, key = rot.get()
                P.op("pe", [], "matmul", bk[:, :], triS[:, :], B.sigw[:, :], start=True, stop=True,
                     rd=["triS", "sigw"], wr=[key])
                P.pe_fence()
                P.op("act", [], "activation", out=B.Pex[:, :], in_=bk[:, :], func=AF.Exp, rd=[key], wr=["Pex"])
                pump(bprev, 2)
                bk, key = rot.get()
                for q in range(4):
                    P.op("pe", [], "matmul", bk[:, q * 2:q * 2 + 2], B.sigw[:, q * 128:(q + 1) * 128], negc[:, :],
                         start=True, stop=True, rd=["sigw", "negc"], wr=[key], sig=(q == 3))
                P.pe_fence()
                P.op("act", [], "activation", out=B.pc[:, :],
                     in_=bk[:, 0:8].rearrange("p (q t) -> p q t", t=2)[:, :, 0], func=AF.Exp, rd=[key], wr=["pc"])
                if DBG.get("stop") == "cum":
                    drain(nxt)
                    continue
                P.op("dve", [], "tensor_tensor", out=B.Rtb[:, :], in0=B.r[:, :], in1=B.Pm[:, :], op=ALU.mult,
                     rd=["r", "Pm"], wr=["Rtb"])
                P.op("dve", [], "tensor_tensor", out=B.Kt[:, :], in0=B.k2[:, :], in1=B.Pinv[:, :], op=ALU.mult,
                     rd=["k2", "Pinv"], wr=["Kt"])
                P.op("dve", [], "tensor_tensor", out=B.Btb[:, :], in0=B.bb[:, :], in1=B.Pinv[:, :], op=ALU.mult,
                     rd=["bb", "Pinv"], wr=["Btb"])
                P.op("dve", [], "scalar_tensor_tensor", out=B.Atb[:, :], in0=B.kkn[:, :], scalar=-1.0,
                     in1=B.Pex[:, :], op0=ALU.mult, op1=ALU.mult, rd=["kkn", "Pex"], wr=["Atb"])
                P.op("pool", [], "tensor_copy", R(B.vr[:, :]), Bv[:, :], rd=[kv], wr=["vr"])
                pump(bprev, 2)
                if DBG.get("stop") == "dec":
                    drain(nxt)
                    continue
                for i_, (src, skey, dkey) in enumerate([(B.Atb, "Atb", "AT"), (B.Rtb, "Rtb", "RT"),
                                                        (B.Btb, "Btb", "BT"), (B.Kt, "Kt", "KT")]):
                    bk, key = rot.get()
                    bkb = bk[:, :].bitcast(BF16)
                    for q in range(4):
                        P.op("pe", [], "transpose", bkb[:, q * 128:(q + 1) * 128], src[:, q * 128:(q + 1) * 128],
                             identb[:, :], rd=[skey, "identb"], wr=[key], sig=(q == 3))
                    view = bkb[:, 0:512].rearrange("p (q t) -> p q t", q=4)
                    if i_ < 2:
                        P.op("act", [], "activation", out=B.ARTd[0:64, :, 0, i_, :], in_=view[0:64], func=AF.Copy,
                             rd=[key], wr=[dkey])
                        P.op("dve", [], "tensor_copy", B.ARTd[64:128, :, 1, i_, :], view[64:128],
                             rd=[key], wr=[dkey])
                    elif i_ == 2:
                        P.op("act", [], "activation", out=B.BT[:, :, :], in_=view, func=AF.Copy, rd=[key], wr=[dkey])
                    else:
                        P.op("dve", [], "tensor_copy", B.KT[:, :, :], view, rd=[key], wr=[dkey])
                    pump(bprev, 2)
                if DBG.get("stop") == "tilde":
                    drain(nxt)
                    continue
                msi4 = M_si[:, :].rearrange("p (b c) -> p b c", b=2).unsqueeze(1).to_broadcast([128, 2, 2, 128])
                ml4 = M_L[:, :].unsqueeze(1).to_broadcast([128, 4, 128])
                for G in range(2):
                    VA, VB, VC = V[G]
                    SA = getattr(B, "SA%d" % G)
                    SB_ = getattr(B, "SB%d" % G)
                    Xb = getattr(B, "Xb%d" % G)
                    XM = getattr(B, "XM%d" % G)
                    for which, lhs, lkey, dst, dkey in [(0, B.BT, "BT", SA, "SA%d" % G), (1, B.KT, "KT", SB_, "SB%d" % G)]:
                        for pi, (bk, key) in enumerate((VA, VB)):
                            q = 2 * G + pi
                            P.op("pe", [], "matmul", bk, R(lhs[:, q, :]),
                                 R(B.ARTd[:, q, :, :, :].rearrange("p e a t -> p (e a t)")), start=True, stop=True,
                                 rd=[lkey, "AT", "RT"], wr=[key])
                            P.op("dve", [], "tensor_tensor", out=R(dst[:, pi * 2:pi * 2 + 2, :, :]),
                                 in0=bk.rearrange("p (a b c) -> p a b c", a=2, b=2), in1=msi4, op=ALU.mult,
                                 rd=[key, "M_si"], wr=[dkey])
                    bk, key = VC
                    for hl in range(4):
                        q = 2 * G + hl // 2
                        e = hl % 2
                        P.op("pe", [], "matmul", bk[:, hl * 128:(hl + 1) * 128], R(B.ARTd[:, q, e, 0, :]),
                             R(B.BT[:, q, :]), start=True, stop=True, rd=["AT", "BT"], wr=[key],
                             sig=(hl == 3))
                    P.op("dve", [], "tensor_tensor", out=Xb[:, :, :], in0=bk.rearrange("p (a c) -> p a c", a=4),
                         in1=ml4, op=ALU.mult, rd=[key, "M_L"], wr=["Xb%d" % G])
                    P.op("act", [], "activation", out=XM[:, :, 0, :], in_=SA[:, :, 0, :], func=AF.Copy,
                         rd=["SA%d" % G], wr=["XTk%d" % G])
                    P.op("dve", [], "tensor_tensor", out=XM[:, :, 1, :], in0=SA[:, :, 0, :],
                         in1=identf[:, :].unsqueeze(1).to_broadcast([128, 4, 128]), op=ALU.add,
                         rd=["SA%d" % G, "identf"], wr=["MTk%d" % G])
                    pump(bprev, 4)
                drain(bprev)
                bprev = None
                if DBG.get("stop") == "amat":
                    drain(nxt)
                    continue
                for lvl in range(0, 7):
                    for G in range(2):
                        VA, VB, VC = V[G]
                        Xb = getattr(B, "Xb%d" % G)
                        XM = getattr(B, "XM%d" % G)
                        kx, kxt, kmt = "Xb%d" % G, "XTk%d" % G, "MTk%d" % G
                        for hl in range(4):
                            bk, key = (VA, VB)[hl // 2]
                            e = hl % 2
                            if lvl == 0:
                                P.op("pe", [], "matmul", bk[:, e * 256:e * 256 + 128], Xb[:, hl, :], XM[:, hl, 0, :],
                                     start=True, stop=True, rd=[kx, kxt], wr=[key], sig=(hl == 3))
                            elif lvl < 6:
                                P.op("pe", [], "matmul", bk[:, e * 256:(e + 1) * 256], Xb[:, hl, :],
                                     XM[:, hl, :, :].rearrange("p a t -> p (a t)"),
                                     start=True, stop=True, rd=[kx, kxt, kmt], wr=[key], sig=(hl == 3))
                            else:
                                P.op("pe", [], "matmul", bk[:, e * 256 + 128:(e + 1) * 256], Xb[:, hl, :],
                                     XM[:, hl, 1, :], start=True, stop=True, rd=[kx, kmt], wr=[key], sig=(hl == 3))
                        if lvl < 6:
                            for hl in range(4):
                                P.op("pe", [], "matmul", VC[0][:, hl * 128:(hl + 1) * 128], XM[:, hl, 0, :],
                                     Xb[:, hl, :], start=True, stop=True, rd=[kx, kxt], wr=[VC[1]], sig=(hl == 3))
                        pump(nxt, 1)
                    for G in range(2):
                        VA, VB, VC = V[G]
                        Xb = getattr(B, "Xb%d" % G)
                        XM = getattr(B, "XM%d" % G)
                        kx, kxt, kmt = "Xb%d" % G, "XTk%d" % G, "MTk%d" % G
                        vab = Vg[G][:, 0:1024].rearrange("p (h a t) -> p h a t", h=4, a=2)
                        if lvl >= 1:
                            P.op("dve", [], "tensor_tensor", out=XM[:, :, 1, :], in0=XM[:, :, 1, :],
                                 in1=vab[:, :, 1, :], op=ALU.add, rd=[kmt, VA[1], VB[1]], wr=[kmt])
                        if lvl < 6:
                            P.op("act", [], "activation", out=XM[:, :, 0, :], in_=vab[:, :, 0, :], func=AF.Copy,
                                 rd=[VA[1], VB[1]], wr=[kxt])
                            P.op("act", [], "activation", out=Xb[:, :, :],
                                 in_=VC[0].rearrange("p (a c) -> p a c", a=4), func=AF.Copy,
                                 rd=[VC[1]], wr=[kx])
                        pump(nxt, 1)
                if DBG.get("stop") == "inv":
                    drain(nxt)
                    continue
                C1, C1k = V[0][0]
                C2, C2k = V[0][1]

                def hd(h):
                    G = h // 4
                    hl = h % 4
                    return G, hl, h // 2, (h % 2) * 64

                for h in range(8):
                    G, hl, q, pb = hd(h)
                    SB_ = getattr(B, "SB%d" % G)
                    P.op("pe", [], "matmul", C1[:, h * 64:(h + 1) * 64], R(B.ARTd[:, q, h % 2, 0, :]),
                         R(B.Hr[:, q, :]), start=True, stop=False, rd=["AT", "Hr"], wr=[C1k], sig=False)
                    P.op("pe", [], "matmul", C1[:, h * 64:(h + 1) * 64], R(SB_[:, hl, 0, :]),
                         R(B.vr[:, h * 64:(h + 1) * 64]), start=False, stop=True, rd=["SB%d" % G, "vr"], wr=[C1k],
                         sig=(h == 7))
                P.op("act", [], "activation", out=B.Zs[:, :, :], in_=C1.rearrange("p (h i) -> p h i", h=8),
                     func=AF.Copy, rd=[C1k], wr=["Zs"])
                pump(nxt, 2)
                for h in range(8):
                    G, hl, q, pb = hd(h)
                    XM = getattr(B, "XM%d" % G)
                    P.op("pe", [], "matmul", C2[:, h * 64:(h + 1) * 64], XM[:, hl, 1, :], B.Zs[:, h, :],
                         start=True, stop=True, rd=["MTk%d" % G, "Zs"], wr=[C2k], sig=(h == 7))
                P.op("dve", [], "tensor_copy", R(B.Us[:, :, :]), C2.rearrange("p (h i) -> p h i", h=8),
                     rd=[C2k], wr=["Us"])
                pump(nxt, 2)
                for h in range(8):
                    G, hl, q, pb = hd(h)
                    SA = getattr(B, "SA%d" % G)
                    SB_ = getattr(B, "SB%d" % G)
                    P.op("pe", [], "matmul", C1[:, h * 64:(h + 1) * 64], R(B.ARTd[:, q, h % 2, 1, :]),
                         R(B.Hr[:, q, :]), start=True, stop=False, rd=["RT", "Hr"], wr=[C1k], sig=False)
                    P.op("pe", [], "matmul", C1[:, h * 64:(h + 1) * 64], R(SA[:, hl, 1, :]), R(B.Us[:, h, :]),
                         start=False, stop=False, rd=["SA%d" % G, "Us"], wr=[C1k], sig=False)
                    P.op("pe", [], "matmul", C1[:, h * 64:(h + 1) * 64], R(SB_[:, hl, 1, :]),
                         R(B.vr[:, h * 64:(h + 1) * 64]), start=False, stop=True, rd=["SB%d" % G, "vr"], wr=[C1k],
                         sig=(h == 7))
                P.op("act", [], "activation", out=B.O[:, :], in_=C1, func=AF.Copy, rd=[C1k], wr=["O"])
                pump(nxt, 2)
                for h in range(8):
                    G, hl, q, pb = hd(h)
                    P.op("pe", [], "matmul", C2[:, h * 64:(h + 1) * 64], B.Btb[:, q * 128:(q + 1) * 128],
                         B.Us[:, h, :], start=True, stop=False, rd=["Btb", "Us"], wr=[C2k], sig=False)
                    P.op("pe", [], "matmul", C2[:, h * 64:(h + 1) * 64], R(B.Kt[:, q * 128:(q + 1) * 128]),
                         R(B.vr[:, h * 64:(h + 1) * 64]), start=False, stop=True, rd=["Kt", "vr"], wr=[C2k],
                         sig=(h == 7))
                for e in range(2):
                    pb = e * 64
                    src = C2[pb:pb + 64, :].rearrange("p (q e i) -> p q e i", q=4, e=2)[:, :, e, :]
                    pcb = B.pc[pb:pb + 64, :].unsqueeze(2).to_broadcast([64, 4, 64])
                    P.op("dve", [], "tensor_tensor", out=B.ht[pb:pb + 64, :, :], in0=src, in1=pcb, op=ALU.mult,
                         rd=[C2k, "pc"], wr=["ht%d" % e])
                    P.op("dve", [], "tensor_tensor", out=B.Hm[pb:pb + 64, :, :], in0=B.Hm[pb:pb + 64, :, :], in1=pcb,
                         op=ALU.mult, rd=["Hm", "pc"], wr=["Hm"])
                    P.op("dve", [], "tensor_tensor", out=B.Hm[pb:pb + 64, :, :], in0=B.Hm[pb:pb + 64, :, :],
                         in1=B.ht[pb:pb + 64, :, :], op=ALU.add, rd=["Hm", "ht%d" % e], wr=["Hm"])
                P.op("act", [], "activation", out=R(B.Hr[:, :, :]), in_=B.Hm[:, :, :], func=AF.Copy,
                     rd=["Hm"], wr=["Hr"])
                if DBG.get("stop") == "chain":
                    drain(nxt)
                    continue
                drain(nxt)
                bprev = back(B, rot, c * 128, 128, par)
                if DBG.get("no_back_overlap"):
                    drain(bprev)
                else:
                    pump(bprev, 3)
            drain(bprev)
            finals(B, rot, 1, OUT["shift_p"], OUT["conv_p"])
            bk, key = rot.get()
            for q in range(4):
                P.op("pe", [], "transpose", bk[:64, q * 128:(q + 1) * 128], B.Hm[:, q, :], identf[:, :],
                     rd=["Hm", "identf"], wr=[key], sig=(q == 3))
            P.op("act", [], "activation", out=B.tmp[:64, :], in_=bk[:64, :], func=AF.Copy, rd=[key], wr=["tmp"])
            P.dma("sp", "st_fin_h", OUT["wkv_p"].rearrange("h i j -> i h j"),
                  B.tmp[:64, :].rearrange("p (h j) -> p h j", h=8), is_out=True, rd=["tmp"])
            P.flush()
            P.barrier()

        with contextlib.ExitStack() as stk:
            if DBG.get("skip_sample"):
                return
            B = NS()
            sb = alloc_common(stk, B, "b")
            sb("wdec", [128, 512])
            sb("na", [128, 512])
            sb("ssh", [16, PSH])
            sb("scv", [32, 512])
            sb("vec", [128, 6, 4, 64])
            sb("ob", [128, 4, 64])
            sb("sa", [128, 16])
            for i in range(2):
                sb("Sb%d" % i, [128, 16, 64])
                sb("St%d" % i, [128, 16, 64])
            banks = [(stk.enter_context(nc.psum_tensor("ms_b%d" % i, [128, 512], F32)), "QB%d" % i) for i in range(4)]
            rot = Rot(banks)
            n = TS
            P.dma("sp", "ld_ssh", B.ssh[:, :], ST["shift"], wr=["ssh"])
            P.dma("sp", "ld_scv", B.scv[:, :], ST["conv"], wr=["scv"])
            psk = ["ps%d" % m for m in range(14)]
            bk, key = rot.get()
            for m in range(14):
                P.op("pe", [], "transpose", bk[:, m * 16:(m + 1) * 16], B.ssh[:, m * 128:(m + 1) * 128],
                     identf[0:16, 0:16], rd=["ssh", "identf"], wr=[key], sig=(m == 13))
            P.op("act", [], "activation", out=B.psT[:, :, 0:16],
                 in_=bk[:, 0:224].rearrange("p (m b) -> p m b", m=14), func=AF.Copy, rd=[key], wr=psk)
            bk, key = rot.get()
            for c in range(4):
                P.op("pe", [], "transpose", bk[:, c * 32:(c + 1) * 32], B.scv[:, c * 128:(c + 1) * 128],
                     identf[0:32, 0:32], rd=["scv", "identf"], wr=[key], sig=(c == 3))
            P.op("act", [], "activation", out=B.u[:, :, 0:32],
                 in_=bk[:, 0:128].rearrange("p (c b) -> p c b", c=4), func=AF.Copy, rd=[key], wr=["u"])
            P.pe_fence()
            for _ in front(B, rot, TP, n, NB, None, 0):
                pass
            P.op("act", [], "activation", out=B.wdec[:n, :], in_=B.sigw[:n, :], func=AF.Exp, scale=-c_dec,
                 rd=["sigw"], wr=["wdec"])
            P.op("dve", [], "tensor_scalar", out=B.na[:n, :], in0=B.kkn[:n, :], scalar1=-1.0, scalar2=None,
                 op0=ALU.mult, rd=["kkn"], wr=["na"])
            for vi, (t_, kname) in enumerate([(B.r, "r"), (B.wdec, "wdec"), (B.k2, "k2"), (B.v0, "v0"), (B.na, "na"),
                                              (B.bb, "bb")]):
                P.dma("sp", "st_vec%d" % vi, SCR["vec"][vi, :, :], t_[:n, :], rd=[kname], wr=["scr_vec"])
            P.dma("sp", "ld_vec", B.vec[:, :, :, :],
                  SCR["vec"].rearrange("v (t b) (h j) -> (b h) v t j", b=NB, h=8), rd=["scr_vec"], wr=["vec"])
            swv = ST["wkv"]
            for ib in range(4):
                Sb = getattr(B, "Sb%d" % (ib % 2))
                St = getattr(B, "St%d" % (ib % 2))
                ks = "Sb%d" % (ib % 2)
                kt = "St%d" % (ib % 2)
                P.dma("sp", "ld_" + ks, Sb[:, :, :], swv[:, ib * 1024:(ib + 1) * 1024].rearrange(
                    "p (i j) -> p i j", i=16), wr=[ks])

                def bj(vi, t):
                    return B.vec[:, vi, t, :].unsqueeze(1).to_broadcast([128, 16, 64])

                for t in range(4):
                    P.op("dve", [], "tensor_tensor", out=St[:, :, :], in0=Sb[:, :, :], in1=bj(4, t), op=ALU.mult,
                         rd=[ks, "vec"], wr=[kt])
                    P.op("dve", [], "tensor_reduce", out=B.sa[:, :], in_=St[:, :, :], axis=AX.X, op=ALU.add,
                         rd=[kt], wr=["sa"])
                    P.op("dve", [], "tensor_tensor", out=Sb[:, :, :], in0=Sb[:, :, :], in1=bj(1, t), op=ALU.mult,
                         rd=[ks, "vec"], wr=[ks])
                    P.op("dve", [], "tensor_tensor", out=St[:, :, :],
                         in0=B.sa[:, :].unsqueeze(2).to_broadcast([128, 16, 64]), in1=bj(5, t), op=ALU.mult,
                         rd=["sa", "vec"], wr=[kt])
                    P.op("dve", [], "tensor_tensor", out=Sb[:, :, :], in0=Sb[:, :, :], in1=St[:, :, :], op=ALU.add,
                         rd=[ks, kt], wr=[ks])
                    P.op("dve", [], "tensor_tensor", out=St[:, :, :],
                         in0=B.vec[:, 3, t, ib * 16:(ib + 1) * 16].unsqueeze(2).to_broadcast([128, 16, 64]),
                         in1=bj(2, t), op=ALU.mult, rd=["vec"], wr=[kt])
                    P.op("dve", [], "tensor_tensor", out=Sb[:, :, :], in0=Sb[:, :, :], in1=St[:, :, :], op=ALU.add,
                         rd=[ks, kt], wr=[ks])
                    P.op("dve", [], "tensor_tensor", out=St[:, :, :], in0=Sb[:, :, :], in1=bj(0, t), op=ALU.mult,
                         rd=[ks, "vec"], wr=[kt])
                    P.op("dve", [], "tensor_reduce", out=B.ob[:, t, ib * 16:(ib + 1) * 16], in_=St[:, :, :],
                         axis=AX.X, op=ALU.add, rd=[kt], wr=["ob"])
                P.dma("sp", "st_" + ks, OUT["wkv_s"][:, ib * 1024:(ib + 1) * 1024],
                      Sb[:, :, :].rearrange("p i j -> p (i j)"), is_out=True, rd=[ks])
            P.dma("sp", "st_ob", SCR["o"], B.ob[:, :, :], rd=["ob"], wr=["scr_o"])
            ov = SCR["o"].rearrange("(b h) t i -> t b h i", h=8)
            for t in range(4):
                P.dma("sp", "ld_O%d" % t, B.O[t * NB:(t + 1) * NB, :].rearrange("p (h i) -> p h i", h=8), ov[t],
                      rd=["scr_o"], wr=["O"])
            for _ in back(B, rot, TP, n, 0):
                pass
            finals(B, rot, NB, OUT["shift_s"], OUT["conv_s"])
            P.flush()
            P.barrier()


def build_program(stage="full"):
    nc = bass.Bass("TRN2", target_bir_lowering=False)

    def din(nm, shape):
        return nc.dram_tensor(nm, list(shape), F32, kind="ExternalInput").ap()

    def dout(nm, shape):
        return nc.dram_tensor(nm, list(shape), F32, kind="ExternalOutput").ap()

    def dscr(nm, shape):
        return nc.dram_tensor(nm, list(shape), F32, kind="Internal").ap()

    x_in = din("x_in", [NTOK, D])
    W = {}
    for nm, shape in [("ffn1_wg", [D, DFF]), ("ffn1_wu", [D, DFF]), ("ffn1_wd", [DFF, D]),
                      ("ffn2_wg", [D, DFF]), ("ffn2_wu", [D, DFF]), ("ffn2_wd", [DFF, D]),
                      ("ln1_g", [1, D]), ("ln1_b", [1, D]), ("ln2_g", [1, D]), ("ln2_b", [1, D]),
                      ("ln3_g", [1, D]), ("ln3_b", [1, D]),
                      ("w_in", [D, PTOT]), ("mu_shift", [1, PSH]), ("w0", [1, 512]), ("w_lora_up", [64, 512]),
                      ("a0", [1, 512]), ("a_lora_up", [64, 512]), ("g_lora_up", [128, 512]),
                      ("k_k", [1, 512]), ("k_a", [1, 512]), ("r_k", [1, 512]), ("lnx_g", [1, 512]),
                      ("lnx_b", [1, 512]), ("conv_w", [3, 512]), ("w_out", [D, D])]:
        W[nm] = din(nm, shape)
    ST = {"wkv": din("state_wkv", [128, 4096]), "shift": din("state_shift", [NB, PSH]),
          "conv": din("state_conv", [2 * NB, 512])}
    OUT = {"y": dout("y_out", [NTOK, D]), "wkv_p": dout("wkv_p", [8, 64, 64]), "shift_p": dout("shift_p", [1, PSH]),
           "conv_p": dout("conv_p", [2, 512]), "wkv_s": dout("wkv_s", [128, 4096]),
           "shift_s": dout("shift_s", [NB, PSH]), "conv_s": dout("conv_s", [2 * NB, 512])}
    SCR = {"x1": dscr("scr_x1", [NTOK, D]), "x2": dscr("scr_x2", [NTOK, D]),
           "vec": dscr("scr_vec", [6, TS, 512]), "o": dscr("scr_o", [128, 4, 64])}

    supers = [(i * 512, 512) for i in range(4)] + [(TP, TS)]
    with contextlib.ExitStack() as stack:
        P = Prog(nc, stack)
        if stage == "mix":
            mixer_phase(P, nc, x_in, OUT["y"], W, ST, OUT, SCR)
        else:
            ffn_phase(P, nc, "f1", x_in, SCR["x1"], W["ffn1_wg"], W["ffn1_wu"], W["ffn1_wd"], W["ln1_g"], W["ln1_b"],
                      supers, [])
            mixer_phase(P, nc, SCR["x1"], SCR["x2"], W, ST, OUT, SCR)
            ffn_phase(P, nc, "f2", SCR["x2"], OUT["y"], W["ffn2_wg"], W["ffn2_wu"], W["ffn2_wd"], W["ln3_g"],
                      W["ln3_b"], supers, [])
        P.flush(final=True)
    return nc


_NC_CACHE = {}


def kernel(**inputs):
    inp = {k: np.asarray(v) for k, v in inputs.items()}
    if "nc" not in _NC_CACHE:
        _NC_CACHE["nc"] = build_program()
    nc = _NC_CACHE["nc"]
    f32 = np.float32
    xp = inp["x_prompt"].astype(f32)
    xs = inp["x_sample"].astype(f32)
    shared = {}
    for nm in ["ffn1_wg", "ffn1_wu", "ffn1_wd", "ffn2_wg", "ffn2_wu", "ffn2_wd", "w_in", "w_lora_up", "a_lora_up",
               "g_lora_up", "conv_w", "w_out"]:
        shared[nm] = np.ascontiguousarray(inp[nm][0], dtype=f32)
    for nm in ["ln1_g", "ln1_b", "ln2_g", "ln2_b", "ln3_g", "ln3_b", "mu_shift", "w0", "a0", "k_k", "k_a",
               "lnx_g", "lnx_b"]:
        shared[nm] = np.ascontiguousarray(inp[nm].reshape(1, -1), dtype=f32)
    shared["r_k"] = np.ascontiguousarray(inp["r_k"].reshape(1, 512), dtype=f32)
    in_maps = []
    for c in range(NCORES):
        sl = slice(NB * c, NB * (c + 1))
        xs_c = xs[sl].transpose(1, 0, 2).reshape(TS, D)
        m = dict(shared)
        m["x_in"] = np.ascontiguousarray(np.concatenate([xp[c], xs_c], 0), dtype=f32)
        m["state_wkv"] = np.ascontiguousarray(inp["state_wkv"][0, sl].reshape(128, 4096), dtype=f32)
        m["state_shift"] = np.ascontiguousarray(inp["state_shift"][0, sl], dtype=f32)
        m["state_conv"] = np.ascontiguousarray(inp["state_conv"][0, sl].transpose(1, 0, 2).reshape(2 * NB, 512), dtype=f32)
        in_maps.append(m)
    res = run_bass_kernel_spmd(nc, in_maps, core_ids=list(range(NCORES)))
    o = res.results
    y_p = np.stack([o[c]["y_out"][:TP] for c in range(NCORES)], 0)
    y_s = np.concatenate([o[c]["y_out"][TP:].reshape(4, NB, D).transpose(1, 0, 2) for c in range(NCORES)], 0)
    wkv_p = np.stack([o[c]["wkv_p"] for c in range(NCORES)], 0)[None]
    shift_p = np.stack([o[c]["shift_p"].reshape(PSH) for c in range(NCORES)], 0)[None]
    conv_p = np.stack([o[c]["conv_p"] for c in range(NCORES)], 0)[None]
    wkv_s = np.concatenate([o[c]["wkv_s"].reshape(NB, 8, 64, 64) for c in range(NCORES)], 0)[None]
    shift_s = np.concatenate([o[c]["shift_s"] for c in range(NCORES)], 0)[None]
    conv_s = np.concatenate([o[c]["conv_s"].reshape(2, NB, 512).transpose(1, 0, 2) for c in range(NCORES)], 0)[None]
    return tuple(np.ascontiguousarray(a, dtype=f32) for a in
                 (y_p, y_s, wkv_p, shift_p, conv_p, wkv_s, shift_s, conv_s))
```

```python
import contextlib
import numpy as np
import concourse.bass as bass
import concourse.mybir as mybir
from concourse.bass_utils import run_bass_kernel_spmd

F32 = mybir.dt.float32
BF16 = mybir.dt.bfloat16
F32R = mybir.dt.float32r
AF = mybir.ActivationFunctionType
ALU = mybir.AluOpType
AX = mybir.AxisListType

NCORES = 8
D = 1024
DFF = 2816
NF = DFF // 128
TP = 2048
TS = 64
NB = 16
NTOK = TP + TS
PSH = 1792
PTOT = 3328
ALPHA = 2.0 ** 0.25
LN_EPS = 1e-5
GN_EPS = 64e-5

ENGS = ["pe", "act", "dve", "pool", "sp"]
PROJ_GROUPS = [(0, 4), (4, 8), (8, 12), (12, 14), (18, 22), (22, 26), (14, 18)]
DBG = {}


class Prog:
    def __init__(self, nc, stack):
        self.nc = nc
        self.stack = stack
        self.ops = {e: [] for e in ENGS}
        self.sems = {}
        self.cnt = {}
        self.waited = {e: {} for e in ENGS}
        self.outstanding = []
        self.wtok = {}
        self.rtoks = {}
        self.pend = {e: ([], []) for e in ENGS}

    def _tdeps(self, rd, wr):
        d = []
        for b in rd:
            d.append(self.wtok.get(b))
        for b in wr:
            d.append(self.wtok.get(b))
            d.extend(self.rtoks.get(b, ()))
        return d

    def _tcommit(self, tok, rd, wr):
        for b in rd:
            self.rtoks.setdefault(b, []).append(tok)
        for b in wr:
            self.wtok[b] = tok
            self.rtoks[b] = []

    def barrier(self):
        toks = [self.last(e) for e in ENGS] + list(self.outstanding)
        for e in ENGS:
            wl = self._waits(e, toks)
            if wl:
                self.ops[e].append((None, wl, None))

    def sem(self, key):
        if key not in self.sems:
            self.sems[key] = self.stack.enter_context(self.nc.semaphore("s_" + str(key)))
            self.cnt[key] = 0
        return self.sems[key]

    def _flat(self, deps, acc):
        if deps is None:
            return
        if isinstance(deps, tuple) and len(deps) == 2 and isinstance(deps[1], int) and not isinstance(deps[0], tuple):
            acc.append(deps)
            return
        for d in deps:
            self._flat(d, acc)

    def _waits(self, eng, deps):
        acc = []
        self._flat(deps, acc)
        w = {}
        for k, n in acc:
            if n > w.get(k, 0):
                w[k] = n
        wl = []
        for k, n in w.items():
            if self.waited[eng].get(k, 0) >= n:
                continue
            self.waited[eng][k] = n
            wl.append((k, n))
        return wl

    def op(self, eng, deps, meth, *args, sig=True, rd=(), wr=(), **kwargs):
        fn = (lambda e, m=meth, a=args, k=kwargs: getattr(e, m)(*a, **k))
        rd = list(rd)
        wr = list(wr)
        wl = self._waits(eng, [deps, self._tdeps(rd, wr)])
        tok = None
        inc = None
        if sig:
            self.sem(eng)
            self.cnt[eng] += 1
            tok = (eng, self.cnt[eng])
            inc = (eng, 1)
            prd, pwr = self.pend[eng]
            self._tcommit(tok, prd + rd, pwr + wr)
            self.pend[eng] = ([], [])
        else:
            self.pend[eng][0].extend(rd)
            self.pend[eng][1].extend(wr)
        self.ops[eng].append((fn, wl, inc))
        return tok

    def dma(self, q, chan, out, in_, deps=(), is_out=False, rd=(), wr=()):
        rd = list(rd)
        wr = list(wr)
        wl = self._waits(q, [deps, self._tdeps(rd, wr)])
        self.sem(chan)
        self.cnt[chan] += 16
        tok = (chan, self.cnt[chan])
        self._tcommit(tok, rd, wr)
        self.ops[q].append((lambda e, o=out, i=in_: e.dma_start(out=o, in_=i), wl, (chan, 16)))
        if is_out:
            self.outstanding.append(tok)
        return tok

    def pe_fence(self):
        tok = self.last("pe")
        wl = self._waits("pe", [tok])
        if wl:
            self.ops["pe"].append((None, wl, None))

    def last(self, eng):
        return (eng, self.cnt.get(eng, 0)) if self.cnt.get(eng, 0) > 0 else None

    def flush(self, final=False):
        nc = self.nc
        ops = self.ops
        sems = self.sems
        if final:
            wl = self._waits("sp", self.outstanding)
            ops["sp"].append((None, wl, None))

        def replay(e, lst):
            for fn, wl, inc in lst:
                for k, n in wl:
                    e.wait_ge(sems[k], n)
                if fn is None:
                    continue
                ins = fn(e)
                if inc is not None:
                    ins.then_inc(sems[inc[0]], inc[1])

        with nc.Block() as block:
            @block.tensor
            def _(e):
                replay(e, ops["pe"])

            @block.scalar
            def _(e):
                replay(e, ops["act"])

            @block.vector
            def _(e):
                replay(e, ops["dve"])

            @block.gpsimd
            def _(e):
                replay(e, ops["pool"])

            @block.sync
            def _(e):
                replay(e, ops["sp"])
        self.ops = {e: [] for e in ENGS}

    def barrier_tokens(self):
        toks = [self.last(e) for e in ENGS if e in self.cnt]
        return [t for t in toks if t is not None]


def ffn_phase(P, nc, name, x_src, x_dst, wg_d, wu_d, wd_d, lng_d, lnb_d, supers, start_deps):
    with contextlib.ExitStack() as stk:
        def sb(nm, shape, dt):
            return stk.enter_context(nc.sbuf_tensor(name + nm, shape, dt))

        def ps(nm, shape, dt):
            return stk.enter_context(nc.psum_tensor(name + nm, shape, dt))

        wg = sb("wg", [128, 8, DFF], BF16)
        wu = sb("wu", [128, 8, DFF], BF16)
        wd = sb("wd", [128, NF, D], BF16)
        xbf = sb("xbf", [128, 4, D], BF16)
        xT = sb("xT", [128, 8, 512], BF16)
        hT = sb("hT", [128, NF, 512], BF16)
        ssb = [sb("ssb%d" % i, [128, 512], BF16) for i in range(2)]
        xf = [sb("xf%d" % i, [128, D], F32) for i in range(2)]
        zt = [sb("z%d" % i, [128, D], F32) for i in range(2)]
        gt = sb("lng", [128, D], F32)
        bt = sb("lnb", [128, D], F32)
        ident = sb("ident", [128, 128], BF16)
        identf = sb("identf", [128, 128], F32)
        stats = sb("stats", [128, 2, 6], F32)
        mv = sb("mv", [128, 2], F32)
        rs = sb("rs", [128, 2], F32)
        mhalf = sb("mhalf", [128, 1], F32)
        pg = [ps("pg%d" % i, [128, 512], F32) for i in range(2)]
        pu = [ps("pu%d" % i, [128, 512], F32) for i in range(2)]
        py = [ps("py%d" % i, [128, 512], F32) for i in range(2)]
        ptr = [ps("ptr%d" % i, [128, 1024], BF16) for i in range(2)]

        sd = list(start_deps)
        t_id0 = P.op("pool", sd, "memset", identf[:], 0.0)
        t_id1 = P.op("pool", [t_id0], "affine_select", out=identf[:], in_=identf[:], pattern=[[-1, 128]],
                     compare_op=ALU.not_equal, fill=1.0, base=0, channel_multiplier=1)
        t_ident = P.op("pool", [t_id1], "tensor_copy", ident[:], identf[:])
        t_mh = P.op("pool", sd, "memset", mhalf[:], -0.5)
        t_g = P.dma("sp", name + "c_g", gt[:], lng_d.partition_broadcast(128), sd)
        t_b = P.dma("sp", name + "c_b", bt[:], lnb_d.partition_broadcast(128), sd)

        wgv = wg_d.rearrange("(k p) f -> p k f", p=128)
        wuv = wu_d.rearrange("(k p) f -> p k f", p=128)
        wdv = wd_d.rearrange("(f p) d -> p f d", p=128)
        fpieces = [(0, 2), (2, 6), (6, 14), (14, 22)]
        t_wg = {}
        t_wu = {}
        t_wd = {}

        def load_x(si, deps):
            tok0, ntok = supers[si]
            nt = (ntok + 127) // 128
            pp = min(ntok, 128)
            src = x_src[tok0:tok0 + ntok, :].rearrange("(t p) d -> p t d", p=pp)
            return P.dma("pool", name + "xbf", xbf[:pp, :nt, :], src, deps)

        x_tok = {}
        x_tok[0] = load_x(0, sd)
        for pi, (a, b) in enumerate(fpieces):
            tg = P.dma("pool", name + "wg%d" % pi, wg[:, :, a * 128:b * 128], wgv[:, :, a * 128:b * 128], sd)
            tu = P.dma("pool", name + "wu%d" % pi, wu[:, :, a * 128:b * 128], wuv[:, :, a * 128:b * 128], sd)
            for f in range(a, b):
                t_wg[f] = tg
                t_wu[f] = tu
        for pi, (a, b) in enumerate([(0, 6), (6, 14), (14, 22)]):
            td = P.dma("pool", name + "wd%d" % pi, wd[:, a:b, :], wdv[:, a:b, :], sd)
            for f in range(a, b):
                t_wd[f] = td

        st = dict(xT_free=[], xbf_free=None, stat_free=None, ep_i=0, gi=0, tri=0)
        hT_read = {}
        pg_free = [None, None]
        pu_free = [None, None]
        py_free = [None, None]
        ptr_free = [None, None]
        ssb_free = [None, None]
        xf_free = [None, None]
        z_free = [None, None]

        def transposes(si):
            tok0, ntok = supers[si]
            nt = (ntok + 127) // 128
            pp = min(ntok, 128)
            evs = []
            pe_last = None
            for kp in range(4):
                b = st["tri"] % 2
                st["tri"] += 1
                for kk in range(2):
                    k = kp * 2 + kk
                    for t in range(nt):
                        last = (kk == 1 and t == nt - 1)
                        pe_last = P.op(
                            "pe", [x_tok[si], t_ident, ptr_free[b]] if (kk == 0 and t == 0) else [],
                            "transpose", ptr[b][:, kk * 512 + t * 128: kk * 512 + t * 128 + pp],
                            xbf[:pp, t, k * 128:(k + 1) * 128], ident[:pp, :pp], sig=last)
                ev = P.op(
                    "act", [pe_last] + (st["xT_free"] if kp == 0 else []), "activation",
                    out=xT[:, kp * 2:kp * 2 + 2, :ntok],
                    in_=ptr[b][:, :].rearrange("p (k n) -> p k n", k=2)[:, :, :ntok], func=AF.Copy)
                ptr_free[b] = ev
                evs.append(ev)
            st["xbf_free"] = pe_last
            return evs

        xT_ready = transposes(0)

        for si, (tok0, ntok) in enumerate(supers):
            nt = (ntok + 127) // 128
            pp = min(ntok, 128)
            h_ready = {}
            gu_last = None
            for f in range(NF):
                b = st["gi"] % 2
                st["gi"] += 1
                for k in range(8):
                    tgk = P.op("pe", [t_wg[f], xT_ready, pg_free[b]] if k == 0 else [], "matmul",
                               pg[b][:, :ntok], wg[:, k, f * 128:(f + 1) * 128], xT[:, k, :ntok],
                               start=(k == 0), stop=(k == 7), sig=(k == 7))
                for k in range(8):
                    tuk = P.op("pe", [t_wu[f], pu_free[b]] if k == 0 else [], "matmul",
                               pu[b][:, :ntok], wu[:, k, f * 128:(f + 1) * 128], xT[:, k, :ntok],
                               start=(k == 0), stop=(k == 7), sig=(k == 7))
                gu_last = tuk
                ts_ = P.op("act", [tgk, ssb_free[b]], "activation",
                           out=ssb[b][:, :ntok], in_=pg[b][:, :ntok], func=AF.Silu)
                pg_free[b] = ts_
                th = P.op("dve", [ts_, tuk, hT_read.get(f)], "scalar_tensor_tensor",
                          out=hT[:, f, :ntok], in0=ssb[b][:, :ntok], scalar=0.5, in1=pu[b][:, :ntok],
                          op0=ALU.mult, op1=ALU.mult)
                ssb_free[b] = th
                pu_free[b] = th
                h_ready[f] = th
            st["xT_free"] = [gu_last]
            if si + 1 < len(supers):
                x_tok[si + 1] = load_x(si + 1, [st["xbf_free"]])
            for t in range(nt):
                xb = st["ep_i"] % 2
                zb = xb
                st["ep_i"] += 1
                txf = P.dma("sp", name + "xf%d" % xb, xf[xb][:pp, :],
                            x_src[tok0 + t * 128: tok0 + t * 128 + pp, :], [xf_free[xb]])
                ylast = []
                for half in range(2):
                    for f in range(NF):
                        ty = P.op("pe", [h_ready[f], t_wd[f]] + ([py_free[half]] if f == 0 else []), "matmul",
                                  py[half][:pp, :], hT[:, f, t * 128:t * 128 + pp],
                                  wd[:, f, half * 512:(half + 1) * 512],
                                  start=(f == 0), stop=(f == NF - 1), sig=(f == NF - 1))
                    ylast.append(ty)
                if t == nt - 1:
                    for f in range(NF):
                        hT_read[f] = ylast[1]
                z = zt[zb]
                tz = []
                for half in range(2):
                    tzz = P.op("dve", [ylast[half], txf, z_free[zb]], "scalar_tensor_tensor",
                               out=z[:pp, half * 512:(half + 1) * 512],
                               in0=xf[xb][:pp, half * 512:(half + 1) * 512],
                               scalar=ALPHA, in1=py[half][:pp, :], op0=ALU.mult, op1=ALU.add)
                    py_free[half] = tzz
                    tz.append(tzz)
                xf_free[xb] = tz[1]
                tst = []
                for half in range(2):
                    tst.append(P.op("dve", [tz[half], st["stat_free"]], "bn_stats",
                                    out=stats[:pp, half, :], in_=z[:pp, half * 512:(half + 1) * 512]))
                tag = P.op("dve", tst, "bn_aggr", out=mv[:pp, :],
                           in_=stats[:pp, :, :].rearrange("p a b -> p (a b)"))
                t1 = P.op("pool", [tag, t_mh], "tensor_scalar", out=rs[:pp, 0:1], in0=mv[:pp, 1:2],
                          scalar1=LN_EPS, scalar2=None, op0=ALU.add)
                t2 = P.op("pool", [t1], "tensor_tensor", out=rs[:pp, 1:2], in0=rs[:pp, 0:1],
                          in1=mhalf[:pp, 0:1], op=ALU.pow)
                tn = P.op("dve", [t2, tag], "tensor_scalar", out=z[:pp, :], in0=z[:pp, :],
                          scalar1=mv[:pp, 0:1], scalar2=rs[:pp, 1:2], op0=ALU.subtract, op1=ALU.mult)
                st["stat_free"] = tn
                tg1 = P.op("pool", [tn, t_g], "tensor_tensor", out=z[:pp, :], in0=z[:pp, :],
                           in1=gt[:pp, :], op=ALU.mult)
                tg2 = P.op("pool", [tg1, t_b], "tensor_tensor", out=z[:pp, :], in0=z[:pp, :],
                           in1=bt[:pp, :], op=ALU.add)
                tout = P.dma("sp", name + "zo%d" % zb, x_dst[tok0 + t * 128: tok0 + t * 128 + pp, :],
                             z[:pp, :], [tg2], is_out=True)
                z_free[zb] = tout
                if t == 0 and si + 1 < len(supers):
                    xT_ready = transposes(si + 1)
        P.flush()


class NS:
    pass


class Rot:
    def __init__(self, banks):
        self.banks = banks
        self.i = 0

    def get(self):
        b, key = self.banks[self.i % len(self.banks)]
        self.i += 1
        return b, key


def rsqrt_pool(P, n, y, x, t, mh, ky, kx, kt):
    P.op("pool", [], "tensor_tensor", out=y, in0=x, in1=mh, op=ALU.pow, rd=[kx, "mh"], wr=[ky])
    P.op("pool", [], "tensor_tensor", out=t, in0=x, in1=y, op=ALU.mult, rd=[kx, ky], wr=[kt])
    P.op("pool", [], "tensor_tensor", out=t, in0=t, in1=y, op=ALU.mult, rd=[kt, ky], wr=[kt])
    P.op("pool", [], "tensor_scalar", out=t, in0=t, scalar1=-0.5, scalar2=1.5, op0=ALU.mult, op1=ALU.add,
         rd=[kt], wr=[kt])
    P.op("pool", [], "tensor_tensor", out=y, in0=y, in1=t, op=ALU.mult, rd=[ky, kt], wr=[ky])


def ln_epilogue(P, B, pp, ybanks, x_res_d, x_dst_d, g_t, b_t, kg, kb, pfx):
    zk = [pfx + "z0", pfx + "z1"]
    if x_res_d is not None:
        P.dma("sp", pfx + "zld", B.z[:pp, :], x_res_d, wr=zk)
    for half in range(2):
        yb, ykey = ybanks[half]
        P.op("dve", [], "scalar_tensor_tensor", out=B.z[:pp, half * 512:(half + 1) * 512],
             in0=B.z[:pp, half * 512:(half + 1) * 512], scalar=ALPHA, in1=yb[:pp, :],
             op0=ALU.mult, op1=ALU.add, rd=[zk[half], ykey], wr=[zk[half]])
    yield
    for half in range(2):
        P.op("dve", [], "bn_stats", out=B.stats[:pp, half, :], in_=B.z[:pp, half * 512:(half + 1) * 512],
             rd=[zk[half]], wr=[pfx + "st%d" % half])
    P.op("dve", [], "bn_aggr", out=B.mv[:pp, :], in_=B.stats[:pp, :, :].rearrange("p a b -> p (a b)"),
         rd=[pfx + "st0", pfx + "st1"], wr=[pfx + "mv"])
    yield
    P.op("pool", [], "tensor_scalar", out=B.rs[:pp, 0:1], in0=B.mv[:pp, 1:2], scalar1=LN_EPS, scalar2=None,
         op0=ALU.add, rd=[pfx + "mv"], wr=[pfx + "rs0"])
    rsqrt_pool(P, 1, B.rs[:pp, 1:2], B.rs[:pp, 0:1], B.rs[:pp, 2:3], B.mh1[:pp, 0:1],
               pfx + "rs1", pfx + "rs0", pfx + "rs2")
    yield
    P.op("dve", [], "tensor_scalar", out=B.z[:pp, :], in0=B.z[:pp, :], scalar1=B.mv[:pp, 0:1],
         scalar2=B.rs[:pp, 1:2], op0=ALU.subtract, op1=ALU.mult,
         rd=[pfx + "mv", pfx + "rs1"] + zk, wr=zk)
    yield
    P.op("pool", [], "tensor_tensor", out=B.z[:pp, :], in0=B.z[:pp, :], in1=g_t[:pp, :], op=ALU.mult,
         rd=zk + [kg], wr=zk)
    P.op("pool", [], "tensor_tensor", out=B.z[:pp, :], in0=B.z[:pp, :], in1=b_t[:pp, :], op=ALU.add,
         rd=zk + [kb], wr=zk)
    yield
    P.dma("sp", pfx + "zo", x_dst_d, B.z[:pp, :], is_out=True, rd=zk)


def mixer_phase(P, nc, x1_d, x2_d, W, ST, OUT, SCR):
    c_dec = float(np.exp(-0.5))
    P.barrier()
    with contextlib.ExitStack() as stk0:
        def sb0(nm, shape, dt=F32):
            return stk0.enter_context(nc.sbuf_tensor("m_" + nm, shape, dt))

        w_in = sb0("w_in", [128, 8, PTOT], BF16)
        w_out = sb0("w_out", [128, 8, D], BF16)
        wlu = sb0("wlu", [128, 512])
        alu = sb0("alu", [128, 512])
        glu = sb0("glu", [128, 512])
        kkb = sb0("kkb", [128, 512])
        kab = sb0("kab", [128, 512])
        rkb = sb0("rkb", [128, 512])
        lgb = sb0("lgb", [128, 512])
        lbb = sb0("lbb", [128, 512])
        g2t = sb0("g2t", [128, D])
        b2t = sb0("b2t", [128, D])
        rows = sb0("rows", [1, 1152])
        identf = sb0("identf", [128, 128])
        identb = sb0("identb", [128, 128], BF16)
        M_si = sb0("M_si", [128, 256])
        M_L = sb0("M_L", [128, 128])
        triI = sb0("triI", [128, 128])
        triS = sb0("triS", [128, 128])
        negc = sb0("negc", [128, 2])
        muF = sb0("muF", [128, 14])
        cwF = sb0("cwF", [128, 12])
        mh = sb0("mh", [128, 8])
        small_a = sb0("small_a", [16, 128])
        small_b = sb0("small_b", [16, 128])

        wiv = W["w_in"].rearrange("(k p) n -> p k n", p=128)
        for gi_, (m0_, m1_) in enumerate(PROJ_GROUPS):
            P.dma("pool", "c_win_g%d" % gi_, w_in[:, :, m0_ * 128:m1_ * 128], wiv[:, :, m0_ * 128:m1_ * 128],
                  wr=["w_in_g%d" % gi_])
        P.dma("pool", "c_wout", w_out[:], W["w_out"].rearrange("(k p) n -> p k n", p=128), wr=["w_out"])
        P.op("pool", [], "memset", identf[:], 0.0, wr=["identf"])
        P.op("pool", [], "affine_select", out=identf[:], in_=identf[:], pattern=[[-1, 128]],
             compare_op=ALU.not_equal, fill=1.0, base=0, channel_multiplier=1, rd=["identf"], wr=["identf"])
        P.op("pool", [], "tensor_copy", identb[:], identf[:], rd=["identf"], wr=["identb"])
        P.op("pool", [], "memset", M_si[:], 1.0, wr=["M_si"])
        P.op("pool", [], "memset", M_L[:], 1.0, wr=["M_L"])
        for j in range(2):
            P.op("pool", [], "affine_select", out=M_si[:, j * 128:(j + 1) * 128], in_=M_si[:, j * 128:(j + 1) * 128],
                 pattern=[[1, 128]], compare_op=(ALU.is_gt if j % 2 == 0 else ALU.is_ge), fill=0.0, base=0,
                 channel_multiplier=-1, rd=["M_si"], wr=["M_si"])
        P.op("pool", [], "affine_select", out=M_L[:, :], in_=M_L[:, :],
             pattern=[[-1, 128]], compare_op=ALU.is_gt, fill=0.0, base=0,
             channel_multiplier=1, rd=["M_L"], wr=["M_L"])
        P.op("pool", [], "memset", triI[:], -c_dec, wr=["triI"])
        P.op("pool", [], "affine_select", out=triI[:], in_=triI[:], pattern=[[1, 128]], compare_op=ALU.is_ge,
             fill=0.0, base=0, channel_multiplier=-1, rd=["triI"], wr=["triI"])
        P.op("pool", [], "memset", triS[:], -c_dec, wr=["triS"])
        P.op("pool", [], "affine_select", out=triS[:], in_=triS[:], pattern=[[1, 128]], compare_op=ALU.is_gt,
             fill=0.0, base=0, channel_multiplier=-1, rd=["triS"], wr=["triS"])
        P.op("pool", [], "memset", negc[:], -c_dec, wr=["negc"])
        P.op("pool", [], "memset", mh[:], -0.5, wr=["mh"])
        P.op("pool", [], "memset", rows[0:1, 0:128], 1.0, wr=["rows_ones"])
        P.dma("sp", "c_w0", rows[0:1, 128:640], W["w0"], wr=["rows_w0"])
        P.dma("sp", "c_a0", rows[0:1, 640:1152], W["a0"], wr=["rows_a0"])
        P.dma("sp", "c_wlu", wlu[0:64, :], W["w_lora_up"], wr=["wlu"])
        P.dma("sp", "c_alu", alu[64:128, :], W["a_lora_up"], wr=["alu"])
        P.dma("sp", "c_glu", glu[:, :], W["g_lora_up"], wr=["glu"])
        for nm, t_, src in [("kkb", kkb, "k_k"), ("kab", kab, "k_a"), ("rkb", rkb, "r_k"), ("lgb", lgb, "lnx_g"),
                            ("lbb", lbb, "lnx_b"), ("g2t", g2t, "ln2_g"), ("b2t", b2t, "ln2_b")]:
            P.dma("sp", "c_" + nm, t_[:], W[src].partition_broadcast(128), wr=[nm])
        P.dma("sp", "c_mu", small_a[0:14, :], W["mu_shift"].rearrange("o (m p) -> (o m) p", p=128), wr=["small_a"])
        P.dma("sp", "c_cw", small_b[0:12, :], W["conv_w"].rearrange("w (c p) -> (w c) p", p=128), wr=["small_b"])

        def alloc_common(stk, B, pfx):
            def sb(nm, shape, dt=F32):
                t_ = stk.enter_context(nc.sbuf_tensor("mx" + pfx + "_" + nm, shape, dt))
                setattr(B, nm, t_)
                return t_
            sb("xbf", [128, D], BF16)
            sb("xT", [128, 8, 128], BF16)
            sb("psT", [128, 14, 144])
            sb("carry", [128, 14, 16])
            sb("cc_sb", [128, 4, 128])
            sb("u", [128, 4, 160])
            sb("zc", [128, 4, 128])
            sb("ucarry", [128, 4, 32])
            sb("dtmp", [128, 128])
            sb("rkvb", [128, 12, 128], BF16)
            sb("tanhwd", [128, 128])
            sb("siggd", [128, 128])
            for nm in ["r", "k", "v0", "v1", "sigw", "alr", "kkn", "tmp", "k2", "bb", "O", "g_sb0", "g_sb1"]:
                sb(nm, [128, 512])
            for nm in ["ss", "inv", "nt", "nt2", "bonus0", "bonus1", "s1", "s2", "mean", "rstd"]:
                sb(nm, [128, 8])
            sb("yrw", [128, 512], BF16)
            sb("ymixT0", [128, 8, 128], BF16)
            sb("ymixT1", [128, 8, 128], BF16)
            sb("z", [128, D])
            sb("stats", [128, 2, 6])
            sb("mv", [128, 2])
            sb("rs", [128, 3])
            B.mh1 = mh
            return sb

        def tr_consts(rot):
            bk, key = rot.get()
            P.op("pe", [], "transpose", bk[:, 0:14], small_a[0:14, :], identf[0:14, 0:14],
                 rd=["small_a", "identf"], wr=[key])
            P.op("act", [], "activation", out=muF[:, :], in_=bk[:, 0:14], func=AF.Copy, rd=[key], wr=["muF"])
            bk, key = rot.get()
            P.op("pe", [], "transpose", bk[:, 0:12], small_b[0:12, :], identf[0:12, 0:12],
                 rd=["small_b", "identf"], wr=[key])
            P.op("act", [], "activation", out=cwF[:, :], in_=bk[:, 0:12], func=AF.Copy, rd=[key], wr=["cwF"])
            P.pe_fence()

        def front(B, rot, tok0, ntok, shift, first, par):
            s2 = 2 * shift
            Bv = getattr(B, "v%d" % par)
            Bg = getattr(B, "g_sb%d" % par)
            Bbon = getattr(B, "bonus%d" % par)
            Bym = getattr(B, "ymixT%d" % par)
            kv, kg, kbon, kymc = "v%d" % par, "g_sb%d" % par, "bonus%d" % par, "ymixT_c%d" % par
            psk = ["ps%d" % m for m in range(14)]
            if first is True:
                P.op("pool", [], "memset", B.psT[:, :, 0:shift], 0.0, wr=psk)
                P.op("pool", [], "memset", B.u[:, :, 0:s2], 0.0, wr=["u"])
            elif first is False:
                P.op("pool", [], "tensor_copy", B.psT[:, :, 0:shift], B.carry[:, :, 0:shift], rd=["carry"], wr=psk)
                P.op("pool", [], "tensor_copy", B.u[:, :, 0:s2], B.ucarry[:, :, 0:s2], rd=["ucarry"], wr=["u"])
            P.dma("pool", "ld_xbf", B.xbf[:ntok, :], x1_d[tok0:tok0 + ntok, :], wr=["xbf"])
            bk, key = rot.get()
            bkb = bk[:, :].bitcast(BF16)
            for k in range(8):
                P.op("pe", [], "transpose", bkb[:, k * 128:k * 128 + ntok], B.xbf[:ntok, k * 128:(k + 1) * 128],
                     identb[:ntok, :ntok], rd=["xbf", "identb"], wr=[key], sig=(k == 7))
            P.op("act", [], "activation", out=B.xT[:, :, :ntok],
                 in_=bkb.rearrange("p (k n) -> p k n", k=8)[:, :, :ntok], func=AF.Copy, rd=[key], wr=["xT"])
            yield
            groups = [(0, 4, "sh"), (4, 8, "sh"), (8, 12, "sh"), (12, 14, "sh"), (18, 22, "cc"), (22, 26, "ch"),
                      (14, 18, "cb")]
            cb_view = None
            cb_key = None
            for gi, (m0, m1, kind) in enumerate(groups):
                bk, key = rot.get()
                for m in range(m0, m1):
                    for k in range(8):
                        P.op("pe", [], "matmul", bk[:, (m - m0) * 128:(m - m0) * 128 + ntok],
                             w_in[:, k, m * 128:(m + 1) * 128], B.xT[:, k, :ntok], start=(k == 0), stop=(k == 7),
                             rd=["w_in_g%d" % gi, "xT"], wr=[key], sig=(m == m1 - 1 and k == 7))
                view = bk[:, :].rearrange("p (m n) -> p m n", n=128)[:, :m1 - m0, :ntok]
                if kind == "sh":
                    dst = B.psT[:, m0:m1, shift:shift + ntok]
                    if gi % 2 == 0:
                        P.op("act", [], "activation", out=dst, in_=view, func=AF.Copy, rd=[key], wr=psk[m0:m1])
                    else:
                        P.op("dve", [], "tensor_copy", dst, view, rd=[key], wr=psk[m0:m1])
                elif kind == "cc":
                    P.op("act", [], "activation", out=B.cc_sb[:, :, :ntok], in_=view, func=AF.Copy,
                         rd=[key], wr=["cc_sb"])
                elif kind == "ch":
                    P.op("dve", [], "tensor_tensor", out=B.u[:, :, s2:s2 + ntok], in0=B.cc_sb[:, :, :ntok], in1=view,
                         op=ALU.mult, rd=["cc_sb", key], wr=["u"])
                else:
                    cb_view = view
                    cb_key = key
                yield
            P.op("pool", [], "tensor_copy", B.carry[:, :, 0:shift], B.psT[:, :, ntok:ntok + shift],
                 rd=psk, wr=["carry"])
            for c in range(4):
                P.op("dve", [], "tensor_scalar", out=B.zc[:, c, :ntok], in0=B.u[:, c, 0:ntok],
                     scalar1=cwF[:, c:c + 1], scalar2=None, op0=ALU.mult, rd=["u", "cwF"], wr=["zc%d" % c])
                for w_ in (1, 2):
                    P.op("dve", [], "scalar_tensor_tensor", out=B.zc[:, c, :ntok],
                         in0=B.u[:, c, w_ * shift:w_ * shift + ntok], scalar=cwF[:, w_ * 4 + c:w_ * 4 + c + 1],
                         in1=B.zc[:, c, :ntok], op0=ALU.mult, op1=ALU.add, rd=["u", "cwF", "zc%d" % c],
                         wr=["zc%d" % c])
                yield
            P.op("dve", [], "tensor_tensor", out=Bym[:, 4:8, :ntok], in0=B.zc[:, :, :ntok], in1=cb_view,
                 op=ALU.mult, rd=["zc0", "zc1", "zc2", "zc3", cb_key], wr=[kymc])
            P.op("pool", [], "tensor_copy", B.ucarry[:, :, 0:s2], B.u[:, :, ntok:ntok + s2], rd=["u"], wr=["ucarry"])
            for m in range(14):
                P.op("dve", [], "tensor_tensor", out=B.dtmp[:, :ntok], in0=B.psT[:, m, 0:ntok],
                     in1=B.psT[:, m, shift:shift + ntok], op=ALU.subtract, rd=[psk[m]], wr=["dtmp"])
                P.op("dve", [], "scalar_tensor_tensor", out=B.psT[:, m, shift:shift + ntok], in0=B.dtmp[:, :ntok],
                     scalar=muF[:, m:m + 1], in1=B.psT[:, m, shift:shift + ntok], op0=ALU.mult, op1=ALU.add,
                     rd=["dtmp", "muF", psk[m]], wr=[psk[m]])
                if m % 3 == 2:
                    yield
            P.op("act", [], "activation", out=B.tanhwd[0:64, :ntok], in_=B.psT[0:64, 12, shift:shift + ntok],
                 func=AF.Tanh, rd=[psk[12]], wr=["tanhwd"])
            P.op("act", [], "activation", out=B.siggd[:, :ntok], in_=B.psT[:, 13, shift:shift + ntok],
                 func=AF.Sigmoid, rd=[psk[13]], wr=["siggd"])
            bkw, keyw = rot.get()
            P.op("pe", [], "matmul", bkw[:ntok, :], rows[0:1, 0:ntok], rows[0:1, 128:640], start=True, stop=False,
                 rd=["rows_ones", "rows_w0"], wr=[keyw], sig=False)
            P.op("pe", [], "matmul", bkw[:ntok, :], B.tanhwd[0:64, :ntok], wlu[0:64, :], start=False, stop=True,
                 rd=["tanhwd", "wlu"], wr=[keyw])
            P.op("act", [], "activation", out=B.sigw[:ntok, :], in_=bkw[:ntok, :], func=AF.Sigmoid,
                 rd=[keyw], wr=["sigw"])
            bka, keya = rot.get()
            P.op("pe", [], "matmul", bka[:ntok, :], rows[0:1, 0:ntok], rows[0:1, 640:1152], start=True, stop=False,
                 rd=["rows_ones", "rows_a0"], wr=[keya], sig=False)
            P.op("pe", [], "matmul", bka[:ntok, :], B.psT[64:128, 12, shift:shift + ntok], alu[64:128, :],
                 start=False, stop=True, rd=[psk[12], "alu"], wr=[keya])
            P.op("act", [], "activation", out=B.alr[:ntok, :], in_=bka[:ntok, :], func=AF.Sigmoid,
                 rd=[keya], wr=["alr"])
            bkg, keyg = rot.get()
            P.op("pe", [], "matmul", bkg[:ntok, :], B.siggd[:, :ntok], glu[:, :], start=True, stop=True,
                 rd=["siggd", "glu"], wr=[keyg])
            P.pe_fence()
            P.op("act", [], "activation", out=Bg[:ntok, :], in_=bkg[:ntok, :], func=AF.Copy,
                 rd=[keyg], wr=[kg])
            yield
            P.op("act", [], "activation", out=B.rkvb[:, :, :ntok], in_=B.psT[:, 0:12, shift:shift + ntok],
                 func=AF.Copy, rd=psk[0:12], wr=["rkvb"])
            for i_, (nm, m0) in enumerate([("r", 0), ("k", 4), ("v", 8)]):
                bk, key = rot.get()
                bkb = bk[:, :].bitcast(BF16)
                for j in range(4):
                    P.op("pe", [], "transpose", bkb[:ntok, j * 128:(j + 1) * 128],
                         B.rkvb[:, m0 + j, :ntok], identb[:, :], rd=["rkvb", "identb"], wr=[key],
                         sig=(j == 3))
                dst = Bv if nm == "v" else getattr(B, nm)
                dk = kv if nm == "v" else nm
                if i_ % 2 == 0:
                    P.op("act", [], "activation", out=dst[:ntok, :], in_=bkb[:ntok, 0:512], func=AF.Copy,
                         rd=[key], wr=[dk])
                else:
                    P.op("dve", [], "tensor_copy", dst[:ntok, :], bkb[:ntok, 0:512], rd=[key], wr=[dk])
                yield
            n = ntok

            def v3(t_):
                return t_[:n, :].rearrange("p (h j) -> p h j", h=8)

            def b3(t_):
                return t_[:n, :].unsqueeze(2).to_broadcast([n, 8, 64])

            P.op("dve", [], "tensor_tensor", out=B.kkn[:n, :], in0=B.k[:n, :], in1=kkb[:n, :], op=ALU.mult,
                 rd=["k", "kkb"], wr=["kkn"])
            P.op("dve", [], "tensor_tensor", out=B.tmp[:n, :], in0=B.kkn[:n, :], in1=B.kkn[:n, :], op=ALU.mult,
                 rd=["kkn"], wr=["tmp"])
            P.op("dve", [], "tensor_reduce", out=B.ss[:n, :], in_=v3(B.tmp), axis=AX.X, op=ALU.add,
                 rd=["tmp"], wr=["ss"])
            P.op("dve", [], "tensor_scalar", out=B.ss[:n, :], in0=B.ss[:n, :], scalar1=1e-24, scalar2=None,
                 op0=ALU.max, rd=["ss"], wr=["ss"])
            yield
            rsqrt_pool(P, 8, B.inv[:n, :], B.ss[:n, :], B.nt[:n, :], mh[:n, :], "inv", "ss", "nt")
            P.op("dve", [], "tensor_tensor", out=v3(B.kkn), in0=v3(B.kkn), in1=b3(B.inv), op=ALU.mult,
                 rd=["kkn", "inv"], wr=["kkn"])
            P.op("dve", [], "scalar_tensor_tensor", out=B.tmp[:n, :], in0=B.alr[:n, :], scalar=-1.0,
                 in1=kab[:n, :], op0=ALU.add, op1=ALU.mult, rd=["alr", "kab"], wr=["tmp"])
            P.op("dve", [], "scalar_tensor_tensor", out=B.k2[:n, :], in0=B.tmp[:n, :], scalar=1.0,
                 in1=B.k[:n, :], op0=ALU.add, op1=ALU.mult, rd=["tmp", "k"], wr=["k2"])
            yield
            P.op("dve", [], "tensor_tensor", out=B.bb[:n, :], in0=B.kkn[:n, :], in1=B.alr[:n, :], op=ALU.mult,
                 rd=["kkn", "alr"], wr=["bb"])
            P.op("dve", [], "tensor_tensor", out=B.tmp[:n, :], in0=B.r[:n, :], in1=B.k2[:n, :], op=ALU.mult,
                 rd=["r", "k2"], wr=["tmp"])
            P.op("dve", [], "tensor_tensor", out=B.tmp[:n, :], in0=B.tmp[:n, :], in1=rkb[:n, :], op=ALU.mult,
                 rd=["tmp", "rkb"], wr=["tmp"])
            P.op("dve", [], "tensor_reduce", out=Bbon[:n, :], in_=v3(B.tmp), axis=AX.X, op=ALU.add,
                 rd=["tmp"], wr=[kbon])
            yield

        def back(B, rot, tok0, ntok, par):
            n = ntok
            Bv = getattr(B, "v%d" % par)
            Bg = getattr(B, "g_sb%d" % par)
            Bbon = getattr(B, "bonus%d" % par)
            Bym = getattr(B, "ymixT%d" % par)
            kv, kg, kbon, kymc, kymr = "v%d" % par, "g_sb%d" % par, "bonus%d" % par, "ymixT_c%d" % par, "ymixT_r%d" % par
            scr = B.z[:, 0:512]
            zk = ["mz0", "mz1"]

            def v3(t_):
                return t_[:n, :].rearrange("p (h j) -> p h j", h=8)

            def b3(t_):
                return t_[:n, :].unsqueeze(2).to_broadcast([n, 8, 64])

            P.op("dve", [], "tensor_reduce", out=B.s1[:n, :], in_=v3(B.O), axis=AX.X, op=ALU.add,
                 rd=["O"], wr=["s1"])
            P.op("dve", [], "tensor_tensor", out=scr[:n, :], in0=B.O[:n, :], in1=B.O[:n, :], op=ALU.mult,
                 rd=["O"], wr=zk)
            P.op("dve", [], "tensor_reduce", out=B.s2[:n, :], in_=v3(scr), axis=AX.X, op=ALU.add,
                 rd=zk, wr=["s2"])
            P.op("dve", [], "tensor_scalar", out=B.mean[:n, :], in0=B.s1[:n, :], scalar1=1.0 / 64, scalar2=None,
                 op0=ALU.mult, rd=["s1"], wr=["mean"])
            P.op("dve", [], "tensor_tensor", out=B.s1[:n, :], in0=B.mean[:n, :], in1=B.mean[:n, :], op=ALU.mult,
                 rd=["mean"], wr=["s1"])
            P.op("dve", [], "scalar_tensor_tensor", out=B.s2[:n, :], in0=B.s2[:n, :], scalar=1.0 / 64,
                 in1=B.s1[:n, :], op0=ALU.mult, op1=ALU.subtract, rd=["s2", "s1"], wr=["s2"])
            P.op("dve", [], "tensor_scalar", out=B.s2[:n, :], in0=B.s2[:n, :], scalar1=GN_EPS, scalar2=None,
                 op0=ALU.add, rd=["s2"], wr=["s2"])
            yield
            rsqrt_pool(P, 8, B.rstd[:n, :], B.s2[:n, :], B.nt2[:n, :], mh[:n, :], "rstd", "s2", "nt2")
            yield
            P.op("dve", [], "tensor_tensor", out=v3(B.O), in0=v3(B.O), in1=b3(B.mean), op=ALU.subtract,
                 rd=["O", "mean"], wr=["O"])
            P.op("dve", [], "tensor_tensor", out=v3(B.O), in0=v3(B.O), in1=b3(B.rstd), op=ALU.mult,
                 rd=["O", "rstd"], wr=["O"])
            P.op("pool", [], "tensor_tensor", out=B.O[:n, :], in0=B.O[:n, :], in1=lgb[:n, :], op=ALU.mult,
                 rd=["O", "lgb"], wr=["O"])
            P.op("pool", [], "tensor_tensor", out=B.O[:n, :], in0=B.O[:n, :], in1=lbb[:n, :], op=ALU.add,
                 rd=["O", "lbb"], wr=["O"])
            yield
            P.op("dve", [], "tensor_tensor", out=v3(scr), in0=v3(Bv), in1=b3(Bbon), op=ALU.mult,
                 rd=[kv, kbon], wr=zk)
            P.op("dve", [], "tensor_tensor", out=B.O[:n, :], in0=B.O[:n, :], in1=scr[:n, :], op=ALU.add,
                 rd=["O"] + zk, wr=["O"])
            P.op("dve", [], "tensor_tensor", out=B.yrw[:n, :], in0=B.O[:n, :], in1=Bg[:n, :], op=ALU.mult,
                 rd=["O", kg], wr=["yrw"])
            yield
            bk, key = rot.get()
            bkb = bk[:, :].bitcast(BF16)
            for q in range(4):
                P.op("pe", [], "transpose", bkb[:, q * 128:q * 128 + n], B.yrw[:n, q * 128:(q + 1) * 128],
                     identb[:n, :n], rd=["yrw", "identb"], wr=[key], sig=(q == 3))
            P.op("act", [], "activation", out=Bym[:, 0:4, :n],
                 in_=bkb[:, 0:512].rearrange("p (q t) -> p q t", q=4)[:, :, :n], func=AF.Copy,
                 rd=[key], wr=[kymr])
            yield
            P.dma("sp", "mzld", B.z[:n, :], x1_d[tok0:tok0 + n, :], wr=zk)
            yield
            ybanks = []
            for half in range(2):
                bk, key = rot.get()
                for k in range(8):
                    P.op("pe", [], "matmul", bk[:n, :], Bym[:, k, :n], w_out[:, k, half * 512:(half + 1) * 512],
                         start=(k == 0), stop=(k == 7), rd=[kymr, kymc, "w_out"], wr=[key], sig=(k == 7))
                ybanks.append((bk, key))
            yield from ln_epilogue(P, B, n, ybanks, None, x2_d[tok0:tok0 + n, :], g2t, b2t, "g2t", "b2t", "m")

        def finals(B, rot, shift, sh_dst, cv_dst):
            s2 = 2 * shift
            for gi, g0 in enumerate(range(0, 14, 4)):
                g1 = min(g0 + 4, 14)
                bk, key = rot.get()
                for m in range(g0, g1):
                    P.op("pe", [], "transpose", bk[:shift, (m - g0) * 128:(m - g0 + 1) * 128], B.carry[:, m, 0:shift],
                         identf[:, :], rd=["carry", "identf"], wr=[key], sig=(m == g1 - 1))
                P.op("act", [], "activation", out=B.tmp[:shift, 0:(g1 - g0) * 128],
                     in_=bk[:shift, 0:(g1 - g0) * 128], func=AF.Copy, rd=[key], wr=["tmp"])
                P.dma("sp", "st_fin_sh%d" % gi, sh_dst[:, g0 * 128:g1 * 128], B.tmp[:shift, 0:(g1 - g0) * 128],
                      is_out=True, rd=["tmp"])
            bk, key = rot.get()
            for c in range(4):
                P.op("pe", [], "transpose", bk[:s2, c * 128:(c + 1) * 128], B.ucarry[:, c, 0:s2], identf[:, :],
                     rd=["ucarry", "identf"], wr=[key], sig=(c == 3))
            P.op("act", [], "activation", out=B.tmp[:s2, :], in_=bk[:s2, :], func=AF.Copy, rd=[key], wr=["tmp"])
            P.dma("sp", "st_fin_cv", cv_dst, B.tmp[:s2, :], is_out=True, rd=["tmp"])

        with contextlib.ExitStack() as stk:
            B = NS()
            sb = alloc_common(stk, B, "a")
            for nm in ["Pm", "Pinv", "Pex"]:
                sb(nm, [128, 512])
            for nm in ["Rtb", "Atb", "Btb", "Kt", "vr"]:
                sb(nm, [128, 512], BF16)
            sb("BT", [128, 4, 128], BF16)
            sb("KT", [128, 4, 128], BF16)
            sb("ARTd", [128, 4, 2, 2, 128], BF16)
            for G in range(2):
                sb("SA%d" % G, [128, 4, 2, 128], BF16)
                sb("SB%d" % G, [128, 4, 2, 128], BF16)
                sb("Xb%d" % G, [128, 4, 128], BF16)
                sb("XM%d" % G, [128, 4, 2, 128], BF16)
            sb("Zs", [128, 8, 64], BF16)
            sb("Us", [128, 8, 64], BF16)
            sb("Hm", [128, 4, 64])
            sb("Hr", [128, 4, 64], BF16)
            sb("pc", [128, 4])
            sb("ht", [128, 4, 64])
            banks = [(stk.enter_context(nc.psum_tensor("mp_b%d" % i, [128, 512], F32)), "PB%d" % i) for i in range(2)]
            rot = Rot(banks[0:2])
            Vg = [stk.enter_context(nc.psum_tensor("mp_v%d" % G, [128, 1536], F32)) for G in range(2)]
            V = [[(Vg[G][:, i * 512:(i + 1) * 512], "V%d_%d" % (G, i)) for i in range(3)] for G in range(2)]
            tr_consts(rot)
            P.op("pool", [], "memset", B.Hm[:], 0.0, wr=["Hm"])
            P.op("pool", [], "memset", B.Hr[:], 0.0, wr=["Hr"])
            P.op("pool", [], "memset", B.ARTd[:], 0.0, wr=["AT", "RT"])

            def R(ap):
                return ap

            def pump(g, n_):
                if g is None:
                    return
                for _ in range(n_):
                    try:
                        next(g)
                    except StopIteration:
                        return

            def drain(g):
                if g is not None:
                    for _ in g:
                        pass

            nch = DBG.get("nchunks", TP // 128)
            drain(front(B, rot, 0, 128, 1, True, 0))
            bprev = None
            for c in range(nch):
                par = c % 2
                Bv = getattr(B, "v%d" % par)
                kv = "v%d" % par
                nxt = front(B, rot, (c + 1) * 128, 128, 1, False, (c + 1) % 2) if c + 1 < nch else None
                if DBG.get("front_only"):
                    drain(nxt)
                    continue
                pump(bprev, 2)
                bk, key = rot.get()
                P.op("pe", [], "matmul", bk[:, :], triI[:, :], B.sigw[:, :], start=True, stop=True,
                     rd=["triI", "sigw"], wr=[key])
                P.pe_fence()
                P.op("act", [], "activation", out=B.Pm[:, :], in_=bk[:, :], func=AF.Exp, rd=[key], wr=["Pm"])
                P.op("act", [], "activation", out=B.Pinv[:, :], in_=bk[:, :], func=AF.Exp, scale=-1.0,
                     rd=[key], wr=["Pinv"])
                bk, key = rot.get()
                P.op("pe", [], "matmul", bk[:, :], triS[:, :], B.sigw[:, :], start=True, stop=True,
                     rd=["triS", "sigw"], wr=[key])
                P.pe_fence()
                P.op("act", [], "activation", out=B.Pex[:, :], in_=bk[:, :], func=AF.Exp, rd=[key], wr=["Pex"])
                pump(bprev, 2)
                bk, key = rot.get()
                for q in range(4):
                    P.op("pe", [], "matmul", bk[:, q * 2:q * 2 + 2], B.sigw[:, q * 128:(q + 1) * 128], negc[:, :],
                         start=True, stop=True, rd=["sigw", "negc"], wr=[key], sig=(q == 3))
                P.pe_fence()
                P.op("act", [], "activation", out=B.pc[:, :],
                     in_=bk[:, 0:8].rearrange("p (q t) -> p q t", t=2)[:, :, 0], func=AF.Exp, rd=[key], wr=["pc"])
                if DBG.get("stop") == "cum":
                    drain(nxt)
                    continue
                P.op("dve", [], "tensor_tensor", out=B.Rtb[:, :], in0=B.r[:, :], in1=B.Pm[:, :], op=ALU.mult,
                     rd=["r", "Pm"], wr=["Rtb"])
                P.op("dve", [], "tensor_tensor", out=B.Kt[:, :], in0=B.k2[:, :], in1=B.Pinv[:, :], op=ALU.mult,
                     rd=["k2", "Pinv"], wr=["Kt"])
                P.op("dve", [], "tensor_tensor", out=B.Btb[:, :], in0=B.bb[:, :], in1=B.Pinv[:, :], op=ALU.mult,
                     rd=["bb", "Pinv"], wr=["Btb"])
                P.op("dve", [], "scalar_tensor_tensor", out=B.Atb[:, :], in0=B.kkn[:, :], scalar=-1.0,
                     in1=B.Pex[:, :], op0=ALU.mult, op1=ALU.mult, rd=["kkn", "Pex"], wr=["Atb"])
                P.op("pool", [], "tensor_copy", R(B.vr[:, :]), Bv[:, :], rd=[kv], wr=["vr"])
                pump(bprev, 2)
                if DBG.get("stop") == "dec":
                    drain(nxt)
                    continue
                for i_, (src, skey, dkey) in enumerate([(B.Atb, "Atb", "AT"), (B.Rtb, "Rtb", "RT"),
                                                        (B.Btb, "Btb", "BT"), (B.Kt, "Kt", "KT")]):
                    bk, key = rot.get()
                    bkb = bk[:, :].bitcast(BF16)
                    for q in range(4):
                        P.op("pe", [], "transpose", bkb[:, q * 128:(q + 1) * 128], src[:, q * 128:(q + 1) * 128],
                             identb[:, :], rd=[skey, "identb"], wr=[key], sig=(q == 3))
                    view = bkb[:, 0:512].rearrange("p (q t) -> p q t", q=4)
                    if i_ < 2:
                        P.op("act", [], "activation", out=B.ARTd[0:64, :, 0, i_, :], in_=view[0:64], func=AF.Copy,
                             rd=[key], wr=[dkey])
                        P.op("dve", [], "tensor_copy", B.ARTd[64:128, :, 1, i_, :], view[64:128],
                             rd=[key], wr=[dkey])
                    elif i_ == 2:
                        P.op("act", [], "activation", out=B.BT[:, :, :], in_=view, func=AF.Copy, rd=[key], wr=[dkey])
                    else:
                        P.op("dve", [], "tensor_copy", B.KT[:, :, :], view, rd=[key], wr=[dkey])
                    pump(bprev, 2)
                if DBG.get("stop") == "tilde":
                    drain(nxt)
                    continue
                msi4 = M_si[:, :].rearrange("p (b c) -> p b c", b=2).unsqueeze(1).to_broadcast([128, 2, 2, 128])
                ml4 = M_L[:, :].unsqueeze(1).to_broadcast([128, 4, 128])
                for G in range(2):
                    VA, VB, VC = V[G]
                    SA = getattr(B, "SA%d" % G)
                    SB_ = getattr(B, "SB%d" % G)
                    Xb = getattr(B, "Xb%d" % G)
                    XM = getattr(B, "XM%d" % G)
                    for which, lhs, lkey, dst, dkey in [(0, B.BT, "BT", SA, "SA%d" % G), (1, B.KT, "KT", SB_, "SB%d" % G)]:
                        for pi, (bk, key) in enumerate((VA, VB)):
                            q = 2 * G + pi
                            P.op("pe", [], "matmul", bk, R(lhs[:, q, :]),
                                 R(B.ARTd[:, q, :, :, :].rearrange("p e a t -> p (e a t)")), start=True, stop=True,
                                 rd=[lkey, "AT", "RT"], wr=[key])
                            P.op("dve", [], "tensor_tensor", out=R(dst[:, pi * 2:pi * 2 + 2, :, :]),
                                 in0=bk.rearrange("p (a b c) -> p a b c", a=2, b=2), in1=msi4, op=ALU.mult,
                                 rd=[key, "M_si"], wr=[dkey])
                    bk, key = VC
                    for hl in range(4):
                        q = 2 * G + hl // 2
                        e = hl % 2
                        P.op("pe", [], "matmul", bk[:, hl * 128:(hl + 1) * 128], R(B.ARTd[:, q, e, 0, :]),
                             R(B.BT[:, q, :]), start=True, stop=True, rd=["AT", "BT"], wr=[key],
                             sig=(hl == 3))
                    P.op("dve", [], "tensor_tensor", out=Xb[:, :, :], in0=bk.rearrange("p (a c) -> p a c", a=4),
                         in1=ml4, op=ALU.mult, rd=[key, "M_L"], wr=["Xb%d" % G])
                    P.op("act", [], "activation", out=XM[:, :, 0, :], in_=SA[:, :, 0, :], func=AF.Copy,
                         rd=["SA%d" % G], wr=["XTk%d" % G])
                    P.op("dve", [], "tensor_tensor", out=XM[:, :, 1, :], in0=SA[:, :, 0, :],
                         in1=identf[:, :].unsqueeze(1).to_broadcast([128, 4, 128]), op=ALU.add,
                         rd=["SA%d" % G, "identf"], wr=["MTk%d" % G])
                    pump(bprev, 4)
                drain(bprev)
                bprev = None
                if DBG.get("stop") == "amat":
                    drain(nxt)
                    continue
                for lvl in range(0, 7):
                    for G in range(2):
                        VA, VB, VC = V[G]
                        Xb = getattr(B, "Xb%d" % G)
                        XM = getattr(B, "XM%d" % G)
                        kx, kxt, kmt = "Xb%d" % G, "XTk%d" % G, "MTk%d" % G
                        for hl in range(4):
                            bk, key = (VA, VB)[hl // 2]
                            e = hl % 2
                            if lvl == 0:
                                P.op("pe", [], "matmul", bk[:, e * 256:e * 256 + 128], Xb[:, hl, :], XM[:, hl, 0, :],
                                     start=True, stop=True, rd=[kx, kxt], wr=[key], sig=(hl == 3))
                            elif lvl < 6:
                                P.op("pe", [], "matmul", bk[:, e * 256:(e + 1) * 256], Xb[:, hl, :],
                                     XM[:, hl, :, :].rearrange("p a t -> p (a t)"),
                                     start=True, stop=True, rd=[kx, kxt, kmt], wr=[key], sig=(hl == 3))
                            else:
                                P.op("pe", [], "matmul", bk[:, e * 256 + 128:(e + 1) * 256], Xb[:, hl, :],
                                     XM[:, hl, 1, :], start=True, stop=True, rd=[kx, kmt], wr=[key], sig=(hl == 3))
                        if lvl < 6:
                            for hl in range(4):
                                P.op("pe", [], "matmul", VC[0][:, hl * 128:(hl + 1) * 128], XM[:, hl, 0, :],
                                     Xb[:, hl, :], start=True, stop=True, rd=[kx, kxt], wr=[VC[1]], sig=(hl == 3))
                        pump(nxt, 1)
                    for G in range(2):
                        VA, VB, VC = V[G]
                        Xb = getattr(B, "Xb%d" % G)
                        XM = getattr(B, "XM%d" % G)
                        kx, kxt, kmt = "Xb%d" % G, "XTk%d" % G, "MTk%d" % G
                        vab = Vg[G][:, 0:1024].rearrange("p (h a t) -> p h a t", h=4, a=2)
                        if lvl >= 1:
                            P.op("dve", [], "tensor_tensor", out=XM[:, :, 1, :], in0=XM[:, :, 1, :],
                                 in1=vab[:, :, 1, :], op=ALU.add, rd=[kmt, VA[1], VB[1]], wr=[kmt])
                        if lvl < 6:
                            P.op("act", [], "activation", out=XM[:, :, 0, :], in_=vab[:, :, 0, :], func=AF.Copy,
                                 rd=[VA[1], VB[1]], wr=[kxt])
                            P.op("act", [], "activation", out=Xb[:, :, :],
                                 in_=VC[0].rearrange("p (a c) -> p a c", a=4), func=AF.Copy,
                                 rd=[VC[1]], wr=[kx])
                        pump(nxt, 1)
                if DBG.get("stop") == "inv":
                    drain(nxt)
                    continue
                C1, C1k = V[0][0]
                C2, C2k = V[0][1]

                def hd(h):
                    G = h // 4
                    hl = h % 4
                    return G, hl, h // 2, (h % 2) * 64

                for h in range(8):
                    G, hl, q, pb = hd(h)
                    SB_ = getattr(B, "SB%d" % G)
                    P.op("pe", [], "matmul", C1[:, h * 64:(h + 1) * 64], R(B.ARTd[:, q, h % 2, 0, :]),
                         R(B.Hr[:, q, :]), start=True, stop=False, rd=["AT", "Hr"], wr=[C1k], sig=False)
                    P.op("pe", [], "matmul", C1[:, h * 64:(h + 1) * 64], R(SB_[:, hl, 0, :]),
                         R(B.vr[:, h * 64:(h + 1) * 64]), start=False, stop=True, rd=["SB%d" % G, "vr"], wr=[C1k],
                         sig=(h == 7))
                P.op("act", [], "activation", out=B.Zs[:, :, :], in_=C1.rearrange("p (h i) -> p h i", h=8),
                     func=AF.Copy, rd=[C1k], wr=["Zs"])
                pump(nxt, 2)
                for h in range(8):
                    G, hl, q, pb = hd(h)
                    XM = getattr(B, "XM%d" % G)
                    P.op("pe", [], "matmul", C2[:, h * 64:(h + 1) * 64], XM[:, hl, 1, :], B.Zs[:, h, :],
                         start=True, stop=True, rd=["MTk%d" % G, "Zs"], wr=[C2k], sig=(h == 7))
                P.op("dve", [], "tensor_copy", R(B.Us[:, :, :]), C2.rearrange("p (h i) -> p h i", h=8),
                     rd=[C2k], wr=["Us"])
                pump(nxt, 2)
                for h in range(8):
                    G, hl, q, pb = hd(h)
                    SA = getattr(B, "SA%d" % G)
                    SB_ = getattr(B, "SB%d" % G)
                    P.op("pe", [], "matmul", C1[:, h * 64:(h + 1) * 64], R(B.ARTd[:, q, h % 2, 1, :]),
                         R(B.Hr[:, q, :]), start=True, stop=False, rd=["RT", "Hr"], wr=[C1k], sig=False)
                    P.op("pe", [], "matmul", C1[:, h * 64:(h + 1) * 64], R(SA[:, hl, 1, :]), R(B.Us[:, h, :]),
                         start=False, stop=False, rd=["SA%d" % G, "Us"], wr=[C1k], sig=False)
                    P.op("pe", [], "matmul", C1[:, h * 64:(h + 1) * 64], R(SB_[:, hl, 1, :]),
                         R(B.vr[:, h * 64:(h + 1) * 64]), start=False, stop=True, rd=["SB%d" % G, "vr"], wr=[C1k],
                         sig=(h == 7))
                P.op("act", [], "activation", out=B.O[:, :], in_=C1, func=AF.Copy, rd=[C1k], wr=["O"])
                pump(nxt, 2)
                for h in range(8):
                    G, hl, q, pb = hd(h)
                    P.op("pe", [], "matmul", C2[:, h * 64:(h + 1) * 64], B.Btb[:, q * 128:(q + 1) * 128],
                         B.Us[:, h, :], start=True, stop=False, rd=["Btb", "Us"], wr=[C2k], sig=False)
                    P.op("pe", [], "matmul", C2[:, h * 64:(h + 1) * 64], R(B.Kt[:, q * 128:(q + 1) * 128]),
                         R(B.vr[:, h * 64:(h + 1) * 64]), start=False, stop=True, rd=["Kt", "vr"], wr=[C2k],
                         sig=(h == 7))
                for e in range(2):
                    pb = e * 64
                    src = C2[pb:pb + 64, :].rearrange("p (q e i) -> p q e i", q=4, e=2)[:, :, e, :]
                    pcb = B.pc[pb:pb + 64, :].unsqueeze(2).to_broadcast([64, 4, 64])
                    P.op("dve", [], "tensor_tensor", out=B.ht[pb:pb + 64, :, :], in0=src, in1=pcb, op=ALU.mult,
                         rd=[C2k, "pc"], wr=["ht%d" % e])
                    P.op("dve", [], "tensor_tensor", out=B.Hm[pb:pb + 64, :, :], in0=B.Hm[pb:pb + 64, :, :], in1=pcb,
                         op=ALU.mult, rd=["Hm", "pc"], wr=["Hm"])
                    P.op("dve", [], "tensor_tensor", out=B.Hm[pb:pb + 64, :, :], in0=B.Hm[pb:pb + 64, :, :],
                         in1=B.ht[pb:pb + 64, :, :], op=ALU.add, rd=["Hm", "ht%d" % e], wr=["Hm"])
                P.op("act", [], "activation", out=R(B.Hr[:, :, :]), in_=B.Hm[:, :, :], func=AF.Copy,
                     rd=["Hm"], wr=["Hr"])
                if DBG.get("stop") == "chain":
                    drain(nxt)
                    continue
                drain(nxt)
                bprev = back(B, rot, c * 128, 128, par)
                if DBG.get("no_back_overlap"):
                    drain(bprev)
                else:
                    pump(bprev, 0)
            drain(bprev)
            finals(B, rot, 1, OUT["shift_p"], OUT["conv_p"])
            bk, key = rot.get()
            for q in range(4):
                P.op("pe", [], "transpose", bk[:64, q * 128:(q + 1) * 128], B.Hm[:, q, :], identf[:, :],
                     rd=["Hm", "identf"], wr=[key], sig=(q == 3))
            P.op("act", [], "activation", out=B.tmp[:64, :], in_=bk[:64, :], func=AF.Copy, rd=[key], wr=["tmp"])
            P.dma("sp", "st_fin_h", OUT["wkv_p"].rearrange("h i j -> i h j"),
                  B.tmp[:64, :].rearrange("p (h j) -> p h j", h=8), is_out=True, rd=["tmp"])
            P.flush()
            P.barrier()

        with contextlib.ExitStack() as stk:
            if DBG.get("skip_sample"):
                return
            B = NS()
            sb = alloc_common(stk, B, "b")
            sb("wdec", [128, 512])
            sb("na", [128, 512])
            sb("ssh", [16, PSH])
            sb("scv", [32, 512])
            sb("vec", [128, 6, 4, 64])
            sb("ob", [128, 4, 64])
            sb("sa", [128, 16])
            for i in range(2):
                sb("Sb%d" % i, [128, 16, 64])
                sb("St%d" % i, [128, 16, 64])
            banks = [(stk.enter_context(nc.psum_tensor("ms_b%d" % i, [128, 512], F32)), "QB%d" % i) for i in range(4)]
            rot = Rot(banks)
            n = TS
            P.dma("sp", "ld_ssh", B.ssh[:, :], ST["shift"], wr=["ssh"])
            P.dma("sp", "ld_scv", B.scv[:, :], ST["conv"], wr=["scv"])
            psk = ["ps%d" % m for m in range(14)]
            bk, key = rot.get()
            for m in range(14):
                P.op("pe", [], "transpose", bk[:, m * 16:(m + 1) * 16], B.ssh[:, m * 128:(m + 1) * 128],
                     identf[0:16, 0:16], rd=["ssh", "identf"], wr=[key], sig=(m == 13))
            P.op("act", [], "activation", out=B.psT[:, :, 0:16],
                 in_=bk[:, 0:224].rearrange("p (m b) -> p m b", m=14), func=AF.Copy, rd=[key], wr=psk)
            bk, key = rot.get()
            for c in range(4):
                P.op("pe", [], "transpose", bk[:, c * 32:(c + 1) * 32], B.scv[:, c * 128:(c + 1) * 128],
                     identf[0:32, 0:32], rd=["scv", "identf"], wr=[key], sig=(c == 3))
            P.op("act", [], "activation", out=B.u[:, :, 0:32],
                 in_=bk[:, 0:128].rearrange("p (c b) -> p c b", c=4), func=AF.Copy, rd=[key], wr=["u"])
            P.pe_fence()
            for _ in front(B, rot, TP, n, NB, None, 0):
                pass
            P.op("act", [], "activation", out=B.wdec[:n, :], in_=B.sigw[:n, :], func=AF.Exp, scale=-c_dec,
                 rd=["sigw"], wr=["wdec"])
            P.op("dve", [], "tensor_scalar", out=B.na[:n, :], in0=B.kkn[:n, :], scalar1=-1.0, scalar2=None,
                 op0=ALU.mult, rd=["kkn"], wr=["na"])
            for vi, (t_, kname) in enumerate([(B.r, "r"), (B.wdec, "wdec"), (B.k2, "k2"), (B.v0, "v0"), (B.na, "na"),
                                              (B.bb, "bb")]):
                P.dma("sp", "st_vec%d" % vi, SCR["vec"][vi, :, :], t_[:n, :], rd=[kname], wr=["scr_vec"])
            P.dma("sp", "ld_vec", B.vec[:, :, :, :],
                  SCR["vec"].rearrange("v (t b) (h j) -> (b h) v t j", b=NB, h=8), rd=["scr_vec"], wr=["vec"])
            swv = ST["wkv"]
            for ib in range(4):
                Sb = getattr(B, "Sb%d" % (ib % 2))
                St = getattr(B, "St%d" % (ib % 2))
                ks = "Sb%d" % (ib % 2)
                kt = "St%d" % (ib % 2)
                P.dma("sp", "ld_" + ks, Sb[:, :, :], swv[:, ib * 1024:(ib + 1) * 1024].rearrange(
                    "p (i j) -> p i j", i=16), wr=[ks])

                def bj(vi, t):
                    return B.vec[:, vi, t, :].unsqueeze(1).to_broadcast([128, 16, 64])

                for t in range(4):
                    P.op("dve", [], "tensor_tensor", out=St[:, :, :], in0=Sb[:, :, :], in1=bj(4, t), op=ALU.mult,
                         rd=[ks, "vec"], wr=[kt])
                    P.op("dve", [], "tensor_reduce", out=B.sa[:, :], in_=St[:, :, :], axis=AX.X, op=ALU.add,
                         rd=[kt], wr=["sa"])
                    P.op("dve", [], "tensor_tensor", out=Sb[:, :, :], in0=Sb[:, :, :], in1=bj(1, t), op=ALU.mult,
                         rd=[ks, "vec"], wr=[ks])
                    P.op("dve", [], "tensor_tensor", out=St[:, :, :],
                         in0=B.sa[:, :].unsqueeze(2).to_broadcast([128, 16, 64]), in1=bj(5, t), op=ALU.mult,
                         rd=["sa", "vec"], wr=[kt])
                    P.op("dve", [], "tensor_tensor", out=Sb[:, :, :], in0=Sb[:, :, :], in1=St[:, :, :], op=ALU.add,
                         rd=[ks, kt], wr=[ks])
                    P.op("dve", [], "tensor_tensor", out=St[:, :, :],
                         in0=B.vec[:, 3, t, ib * 16:(ib + 1) * 16].unsqueeze(2).to_broadcast([128, 16, 64]),
                         in1=bj(2, t), op=ALU.mult, rd=["vec"], wr=[kt])
                    P.op("dve", [], "tensor_tensor", out=Sb[:, :, :], in0=Sb[:, :, :], in1=St[:, :, :], op=ALU.add,
                         rd=[ks, kt], wr=[ks])
                    P.op("dve", [], "tensor_tensor", out=St[:, :, :], in0=Sb[:, :, :], in1=bj(0, t), op=ALU.mult,
                         rd=[ks, "vec"], wr=[kt])
                    P.op("dve", [], "tensor_reduce", out=B.ob[:, t, ib * 16:(ib + 1) * 16], in_=St[:, :, :],
                         axis=AX.X, op=ALU.add, rd=[kt], wr=["ob"])
                P.dma("sp", "st_" + ks, OUT["wkv_s"][:, ib * 1024:(ib + 1) * 1024],
                      Sb[:, :, :].rearrange("p i j -> p (i j)"), is_out=True, rd=[ks])
            P.dma("sp", "st_ob", SCR["o"], B.ob[:, :, :], rd=["ob"], wr=["scr_o"])
            ov = SCR["o"].rearrange("(b h) t i -> t b h i", h=8)
            for t in range(4):
                P.dma("sp", "ld_O%d" % t, B.O[t * NB:(t + 1) * NB, :].rearrange("p (h i) -> p h i", h=8), ov[t],
                      rd=["scr_o"], wr=["O"])
            for _ in back(B, rot, TP, n, 0):
                pass
            finals(B, rot, NB, OUT["shift_s"], OUT["conv_s"])
            P.flush()
            P.barrier()


def build_program(stage="full"):
    nc = bass.Bass("TRN2", target_bir_lowering=False)

    def din(nm, shape):
        return nc.dram_tensor(nm, list(shape), F32, kind="ExternalInput").ap()

    def dout(nm, shape):
        return nc.dram_tensor(nm, list(shape), F32, kind="ExternalOutput").ap()

    def dscr(nm, shape):
        return nc.dram_tensor(nm, list(shape), F32, kind="Internal").ap()

    x_in = din("x_in", [NTOK, D])
    W = {}
    for nm, shape in [("ffn1_wg", [D, DFF]), ("ffn1_wu", [D, DFF]), ("ffn1_wd", [DFF, D]),
                      ("ffn2_wg", [D, DFF]), ("ffn2_wu", [D, DFF]), ("ffn2_wd", [DFF, D]),
                      ("ln1_g", [1, D]), ("ln1_b", [1, D]), ("ln2_g", [1, D]), ("ln2_b", [1, D]),
                      ("ln3_g", [1, D]), ("ln3_b", [1, D]),
                      ("w_in", [D, PTOT]), ("mu_shift", [1, PSH]), ("w0", [1, 512]), ("w_lora_up", [64, 512]),
                      ("a0", [1, 512]), ("a_lora_up", [64, 512]), ("g_lora_up", [128, 512]),
                      ("k_k", [1, 512]), ("k_a", [1, 512]), ("r_k", [1, 512]), ("lnx_g", [1, 512]),
                      ("lnx_b", [1, 512]), ("conv_w", [3, 512]), ("w_out", [D, D])]:
        W[nm] = din(nm, shape)
    ST = {"wkv": din("state_wkv", [128, 4096]), "shift": din("state_shift", [NB, PSH]),
          "conv": din("state_conv", [2 * NB, 512])}
    OUT = {"y": dout("y_out", [NTOK, D]), "wkv_p": dout("wkv_p", [8, 64, 64]), "shift_p": dout("shift_p", [1, PSH]),
           "conv_p": dout("conv_p", [2, 512]), "wkv_s": dout("wkv_s", [128, 4096]),
           "shift_s": dout("shift_s", [NB, PSH]), "conv_s": dout("conv_s", [2 * NB, 512])}
    SCR = {"x1": dscr("scr_x1", [NTOK, D]), "x2": dscr("scr_x2", [NTOK, D]),
           "vec": dscr("scr_vec", [6, TS, 512]), "o": dscr("scr_o", [128, 4, 64])}

    supers = [(i * 512, 512) for i in range(4)] + [(TP, TS)]
    with contextlib.ExitStack() as stack:
        P = Prog(nc, stack)
        if stage == "mix":
            mixer_phase(P, nc, x_in, OUT["y"], W, ST, OUT, SCR)
        else:
            ffn_phase(P, nc, "f1", x_in, SCR["x1"], W["ffn1_wg"], W["ffn1_wu"], W["ffn1_wd"], W["ln1_g"], W["ln1_b"],
                      supers, [])
            mixer_phase(P, nc, SCR["x1"], SCR["x2"], W, ST, OUT, SCR)
            ffn_phase(P, nc, "f2", SCR["x2"], OUT["y"], W["ffn2_wg"], W["ffn2_wu"], W["ffn2_wd"], W["ln3_g"],
                      W["ln3_b"], supers, [])
        P.flush(final=True)
    return nc


_NC_CACHE = {}


def kernel(**inputs):
    inp = {k: np.asarray(v) for k, v in inputs.items()}
    if "nc" not in _NC_CACHE:
        _NC_CACHE["nc"] = build_program()
    nc = _NC_CACHE["nc"]
    f32 = np.float32
    xp = inp["x_prompt"].astype(f32)
    xs = inp["x_sample"].astype(f32)
    shared = {}
    for nm in ["ffn1_wg", "ffn1_wu", "ffn1_wd", "ffn2_wg", "ffn2_wu", "ffn2_wd", "w_in", "w_lora_up", "a_lora_up",
               "g_lora_up", "conv_w", "w_out"]:
        shared[nm] = np.ascontiguousarray(inp[nm][0], dtype=f32)
    for nm in ["ln1_g", "ln1_b", "ln2_g", "ln2_b", "ln3_g", "ln3_b", "mu_shift", "w0", "a0", "k_k", "k_a",
               "lnx_g", "lnx_b"]:
        shared[nm] = np.ascontiguousarray(inp[nm].reshape(1, -1), dtype=f32)
    shared["r_k"] = np.ascontiguousarray(inp["r_k"].reshape(1, 512), dtype=f32)
    in_maps = []
    for c in range(NCORES):
        sl = slice(NB * c, NB * (c + 1))
        xs_c = xs[sl].transpose(1, 0, 2).reshape(TS, D)
        m = dict(shared)
        m["x_in"] = np.ascontiguousarray(np.concatenate([xp[c], xs_c], 0), dtype=f32)
        m["state_wkv"] = np.ascontiguousarray(inp["state_wkv"][0, sl].reshape(128, 4096), dtype=f32)
        m["state_shift"] = np.ascontiguousarray(inp["state_shift"][0, sl], dtype=f32)
        m["state_conv"] = np.ascontiguousarray(inp["state_conv"][0, sl].transpose(1, 0, 2).reshape(2 * NB, 512), dtype=f32)
        in_maps.append(m)
    res = run_bass_kernel_spmd(nc, in_maps, core_ids=list(range(NCORES)))
    o = res.results
    y_p = np.stack([o[c]["y_out"][:TP] for c in range(NCORES)], 0)
    y_s = np.concatenate([o[c]["y_out"][TP:].reshape(4, NB, D).transpose(1, 0, 2) for c in range(NCORES)], 0)
    wkv_p = np.stack([o[c]["wkv_p"] for c in range(NCORES)], 0)[None]
    shift_p = np.stack([o[c]["shift_p"].reshape(PSH) for c in range(NCORES)], 0)[None]
    conv_p = np.stack([o[c]["conv_p"] for c in range(NCORES)], 0)[None]
    wkv_s = np.concatenate([o[c]["wkv_s"].reshape(NB, 8, 64, 64) for c in range(NCORES)], 0)[None]
    shift_s = np.concatenate([o[c]["shift_s"] for c in range(NCORES)], 0)[None]
    conv_s = np.concatenate([o[c]["conv_s"].reshape(2, NB, 512).transpose(1, 0, 2) for c in range(NCORES)], 0)[None]
    return tuple(np.ascontiguousarray(a, dtype=f32) for a in
                 (y_p, y_s, wkv_p, shift_p, conv_p, wkv_s, shift_s, conv_s))
```
